# Optimizing a Trainium2 kernel written in Bass

```python
import math
import jax, jax.numpy as jnp
from jax import lax
import numpy as np

D_MODEL = 2048
BATCH = 16
SEQ = 256
DEPTH = 2
DEC_BATCH = 8
DEC_SEQ = 2048
PAST_LEN = 512

GRID_W = 64
D_A = D_MODEL // 2
EMB_DIM = 33
N_BANDS = (EMB_DIM - 1) // 2
FILTER_FF = 64
DECAY_TARGET = 1e-2
FAST_DECAY_PCT = 0.3
SLOW_DECAY_PCT = 1.5
MAX_DECAY = math.log(DECAY_TARGET) / FAST_DECAY_PCT
MIN_DECAY = math.log(DECAY_TARGET) / SLOW_DECAY_PCT
N_HEADS = 8
QK_NOPE = 128
ROPE_DIM = 64
V_DIM = 128
Q_LORA = D_MODEL // 4
KV_LORA = D_MODEL // 8
D_B = N_HEADS * V_DIM
ROPE_THETA = 10000.0
ROPE_HALF = ROPE_DIM // 2
AXIS_PAIRS = ROPE_HALF // 2
ATTN_SCALE = (QK_NOPE + ROPE_DIM) ** -0.5
Q_BLOCK = 128
D_C = D_MODEL // 2
D_FF = 5632
EPS = 1e-6
N_IN = 3 * D_A + Q_LORA + KV_LORA + ROPE_DIM + 3 * D_C + 3 * D_MODEL
SPLIT_IDX = (3 * D_A,
             3 * D_A + Q_LORA,
             3 * D_A + Q_LORA + KV_LORA,
             3 * D_A + Q_LORA + KV_LORA + ROPE_DIM,
             3 * D_A + Q_LORA + KV_LORA + ROPE_DIM + 3 * D_C)

kernel_name = "hybrid_flow_prefix_hyena_mla_shortconv"


def rmsnorm(x, g):
    xf = x.astype(jnp.float32)
    y = xf * lax.rsqrt(jnp.mean(xf * xf, axis=-1, keepdims=True) + EPS)
    return y.astype(x.dtype) * g


def dwconv3(x, w):
    xp = jnp.pad(x, ((0, 0), (1, 1), (0, 0)))
    return xp[:, :-2] * w[0] + x * w[1] + xp[:, 2:] * w[2]


def axial_rope(L):
    rows = L // GRID_W
    row = jnp.repeat(jnp.arange(rows, dtype=jnp.float32), GRID_W)
    col = jnp.tile(jnp.arange(GRID_W, dtype=jnp.float32), rows)
    inv = ROPE_THETA ** (-jnp.arange(AXIS_PAIRS, dtype=jnp.float32) / AXIS_PAIRS)
    ang = jnp.concatenate([row[:, None] * inv, col[:, None] * inv], axis=-1)
    return jnp.cos(ang), jnp.sin(ang)


def apply_rope(x, cos, sin):
    cos = cos.astype(x.dtype)
    sin = sin.astype(x.dtype)
    x1, x2 = x[..., :ROPE_HALF], x[..., ROPE_HALF:]
    return jnp.concatenate([x1 * cos - x2 * sin, x1 * sin + x2 * cos], axis=-1)


def hyena_filter(L, lp):
    f32 = jnp.float32
    t_idx = jnp.arange(L, dtype=f32)
    t_norm = t_idx / max(L - 1, 1)
    w = 2.0 * math.pi * t_idx / L
    bands = jnp.linspace(1e-4, N_BANDS - 1, N_BANDS, dtype=f32)
    ang = w[:, None] * bands[None, :]
    z = jnp.concatenate([t_norm[:, None], jnp.cos(ang), -jnp.sin(ang)], axis=-1)
    freq = lp['hy_f_freq'].astype(f32)
    hdn = jnp.sin(freq[0] * (z @ lp['hy_f_w1'].astype(f32) + lp['hy_f_b1'].astype(f32)))
    hdn = jnp.sin(freq[1] * (hdn @ lp['hy_f_w2'].astype(f32) + lp['hy_f_b2'].astype(f32)))
    hf = hdn @ lp['hy_f_w3'].astype(f32) + lp['hy_f_b3'].astype(f32)
    deltas = jnp.abs(jnp.linspace(MIN_DECAY, MAX_DECAY, D_A, dtype=f32))
    window = jnp.exp(-t_norm[:, None] * deltas[None, :])
    hf = hf * jnp.concatenate([window, window], axis=-1)
    h_fwd, h_bwd = hf[:, :D_A], hf[:, D_A:]
    k = jnp.concatenate([h_fwd, jnp.zeros((1, D_A), f32), h_bwd[:0:-1]], axis=0)
    return k / jnp.sum(jnp.abs(k), axis=0, keepdims=True)


def long_conv(u, k):
    L = u.shape[1]
    uf = jnp.fft.rfft(u.astype(jnp.float32), n=2 * L, axis=1)
    kf = jnp.fft.rfft(k, n=2 * L, axis=0)
    y = jnp.fft.irfft(uf * kf[None], n=2 * L, axis=1)[:, :L]
    return y.astype(u.dtype)


def mla_attend(q_lat, q_pe, keys_c, keys_pe):
    B, Lq, H, C = q_lat.shape
    nb = Lq // Q_BLOCK
    qlb = q_lat.reshape(B, nb, Q_BLOCK, H, C).swapaxes(0, 1)
    qpb = q_pe.reshape(B, nb, Q_BLOCK, H, ROPE_DIM).swapaxes(0, 1)

    def block(args):
        ql, qp = args
        s = (jnp.einsum('bqhc,bkc->bhqk', ql, keys_c)
             + jnp.einsum('bqhr,bkr->bhqk', qp, keys_pe)).astype(jnp.float32) * ATTN_SCALE
        p = jax.nn.softmax(s, axis=-1).astype(keys_c.dtype)
        return jnp.einsum('bhqk,bkc->bqhc', p, keys_c)

    out = lax.map(block, (qlb, qpb))
    return out.swapaxes(0, 1).reshape(B, Lq, H, C)


def token_mix(h, lp, rope, ctx_kv):
    Bsz, L, _ = h.shape
    proj = h @ lp['w_in']
    hy_in, cq, ckv, kpe, sc_in, gates = jnp.split(proj, SPLIT_IDX, axis=-1)
    hy = dwconv3(hy_in, lp['hy_conv_w']) + lp['hy_conv_b']
    x0, x1, v = jnp.split(hy, 3, axis=-1)
    zz = x1 * v
    y_a = x0 * (long_conv(zz, hyena_filter(L, lp)) + zz * lp['hy_bias'])
    q = (rmsnorm(cq, lp['q_norm']) @ lp['w_uq']).reshape(Bsz, L, N_HEADS, QK_NOPE + ROPE_DIM)
    q_nope, q_pe = q[..., :QK_NOPE], q[..., QK_NOPE:]
    ckv_n = rmsnorm(ckv, lp['kv_norm'])
    if rope is not None:
        cos, sin = rope
        q_pe = apply_rope(q_pe, cos[:, None, :], sin[:, None, :])
        kpe_r = apply_rope(kpe, cos, sin)
    else:
        kpe_r = kpe
    if ctx_kv is not None:
        keys_c = jnp.concatenate([ckv_n, ctx_kv[0]], axis=1)
        keys_pe = jnp.concatenate([kpe_r, ctx_kv[1]], axis=1)
    else:
        keys_c, keys_pe = ckv_n, kpe_r
    w_ukv = lp['w_ukv'].reshape(KV_LORA, N_HEADS, QK_NOPE + V_DIM)
    q_lat = jnp.einsum('blhn,chn->blhc', q_nope, w_ukv[..., :QK_NOPE])
    o_lat = mla_attend(q_lat, q_pe, keys_c, keys_pe)
    y_b = jnp.einsum('blhc,chv->blhv', o_lat, w_ukv[..., QK_NOPE:]).reshape(Bsz, L, D_B)
    b_g, c_g, u = jnp.split(sc_in, 3, axis=-1)
    y_c = b_g * dwconv3(c_g * u, lp['sc_conv_w'])
    g_a, g_b, g_c = jnp.split(gates, 3, axis=-1)
    m = (jax.nn.sigmoid(g_a) * (y_a @ lp['w_br_a'])
         + jax.nn.sigmoid(g_b) * (y_b @ lp['w_br_b'])
         + jax.nn.sigmoid(g_c) * (y_c @ lp['w_br_c']))
    return m @ lp['w_o'], (ckv_n, kpe)


def conv_ffn(h, lp):
    uu = dwconv3(h @ lp['ffn_up'], lp['ffn_conv_w']) + lp['ffn_conv_b']
    g, v = jnp.split(uu, 2, axis=-1)
    return (jax.nn.silu(g) * v) @ lp['ffn_down']


def trunk_layer(x, cvec, lp, rope, ctx_kv):
    mod = jax.nn.silu(cvec) @ lp['ada_w'] + lp['ada_b']
    sh1, sc1, g1, sh2, sc2, g2 = jnp.split(mod[:, None, :], 6, axis=-1)
    h = rmsnorm(x, lp['norm_mix_pre']) * (1.0 + sc1) + sh1
    o, kv = token_mix(h, lp, rope, ctx_kv)
    x = x + g1 * rmsnorm(o, lp['norm_mix_post'])
    h = rmsnorm(x, lp['norm_ffn_pre']) * (1.0 + sc2) + sh2
    x = x + g2 * rmsnorm(conv_ffn(h, lp), lp['norm_ffn_post'])
    return x, kv


def setup_inputs(seed: int = 0) -> dict:
    key = jax.random.key(seed)
    ks = iter(jax.random.split(key, 64))

    def nrm(shape, scale):
        return jax.random.normal(next(ks), shape, jnp.float32) * scale

    def gain(shape):
        return 1.0 + nrm(shape, 0.05)

    return {
        'x_prompt': nrm((BATCH, SEQ, D_MODEL), 1.0),
        'x_sample': nrm((DEC_BATCH, DEC_SEQ, D_MODEL), 1.0),
        'c': nrm((DEC_BATCH, D_MODEL), 1.0),
        'cache_ckv': nrm((DEC_BATCH, DEPTH, PAST_LEN, KV_LORA), 1.0),
        'cache_kpe': nrm((DEC_BATCH, DEPTH, PAST_LEN, ROPE_DIM), 1.0),
        'c_ctx': nrm((D_MODEL,), 1.0),
        'ada_w': nrm((DEPTH, D_MODEL, 6 * D_MODEL), 0.5 * D_MODEL ** -0.5),
        'ada_b': nrm((DEPTH, 6 * D_MODEL), 0.01),
        'norm_mix_pre': gain((DEPTH, D_MODEL)),
        'norm_mix_post': gain((DEPTH, D_MODEL)),
        'norm_ffn_pre': gain((DEPTH, D_MODEL)),
        'norm_ffn_post': gain((DEPTH, D_MODEL)),
        'w_in': nrm((DEPTH, D_MODEL, N_IN), D_MODEL ** -0.5),
        'hy_conv_w': nrm((DEPTH, 3, 3 * D_A), 3 ** -0.5),
        'hy_conv_b': nrm((DEPTH, 3 * D_A), 0.01),
        'hy_f_w1': nrm((DEPTH, EMB_DIM, FILTER_FF), EMB_DIM ** -0.5),
        'hy_f_b1': nrm((DEPTH, FILTER_FF), 0.01),
        'hy_f_w2': nrm((DEPTH, FILTER_FF, FILTER_FF), FILTER_FF ** -0.5),
        'hy_f_b2': nrm((DEPTH, FILTER_FF), 0.01),
        'hy_f_w3': nrm((DEPTH, FILTER_FF, 2 * D_A), FILTER_FF ** -0.5),
        'hy_f_b3': nrm((DEPTH, 2 * D_A), 0.01),
        'hy_f_freq': gain((DEPTH, 2, FILTER_FF)),
        'hy_bias': nrm((DEPTH, D_A), 0.1),
        'q_norm': gain((DEPTH, Q_LORA)),
        'kv_norm': gain((DEPTH, KV_LORA)),
        'w_uq': nrm((DEPTH, Q_LORA, N_HEADS * (QK_NOPE + ROPE_DIM)), Q_LORA ** -0.5),
        'w_ukv': nrm((DEPTH, KV_LORA, N_HEADS * (QK_NOPE + V_DIM)), KV_LORA ** -0.5),
        'sc_conv_w': nrm((DEPTH, 3, D_C), 3 ** -0.5),
        'w_br_a': nrm((DEPTH, D_A, D_MODEL), D_A ** -0.5),
        'w_br_b': nrm((DEPTH, D_B, D_MODEL), D_B ** -0.5),
        'w_br_c': nrm((DEPTH, D_C, D_MODEL), D_C ** -0.5),
        'w_o': nrm((DEPTH, D_MODEL, D_MODEL), D_MODEL ** -0.5),
        'ffn_up': nrm((DEPTH, D_MODEL, 2 * D_FF), D_MODEL ** -0.5),
        'ffn_conv_w': nrm((DEPTH, 3, 2 * D_FF), 3 ** -0.5),
        'ffn_conv_b': nrm((DEPTH, 2 * D_FF), 0.01),
        'ffn_down': nrm((DEPTH, D_FF, D_MODEL), D_FF ** -0.5),
    }


def reference(x_prompt, x_sample, c, cache_ckv, cache_kpe, c_ctx, ada_w, ada_b,
              norm_mix_pre, norm_mix_post, norm_ffn_pre, norm_ffn_post, w_in,
              hy_conv_w, hy_conv_b, hy_f_w1, hy_f_b1, hy_f_w2, hy_f_b2, hy_f_w3, hy_f_b3,
              hy_f_freq, hy_bias, q_norm, kv_norm, w_uq, w_ukv, sc_conv_w,
              w_br_a, w_br_b, w_br_c, w_o, ffn_up, ffn_conv_w, ffn_conv_b, ffn_down):
    rope = axial_rope(x_sample.shape[1])
    xp, xs = x_prompt, x_sample
    ckv_list, kpe_list = [], []
    for l in range(DEPTH):
        lp = dict(ada_w=ada_w[l], ada_b=ada_b[l],
                  norm_mix_pre=norm_mix_pre[l], norm_mix_post=norm_mix_post[l],
                  norm_ffn_pre=norm_ffn_pre[l], norm_ffn_post=norm_ffn_post[l],
                  w_in=w_in[l], hy_conv_w=hy_conv_w[l], hy_conv_b=hy_conv_b[l],
                  hy_f_w1=hy_f_w1[l], hy_f_b1=hy_f_b1[l], hy_f_w2=hy_f_w2[l], hy_f_b2=hy_f_b2[l],
                  hy_f_w3=hy_f_w3[l], hy_f_b3=hy_f_b3[l], hy_f_freq=hy_f_freq[l], hy_bias=hy_bias[l],
                  q_norm=q_norm[l], kv_norm=kv_norm[l], w_uq=w_uq[l], w_ukv=w_ukv[l],
                  sc_conv_w=sc_conv_w[l], w_br_a=w_br_a[l], w_br_b=w_br_b[l], w_br_c=w_br_c[l],
                  w_o=w_o[l], ffn_up=ffn_up[l], ffn_conv_w=ffn_conv_w[l], ffn_conv_b=ffn_conv_b[l],
                  ffn_down=ffn_down[l])
        xp, (ckv_l, kpe_l) = trunk_layer(xp, c_ctx[None, :], lp, None, None)
        ckv_list.append(ckv_l)
        kpe_list.append(kpe_l)
        xs, _ = trunk_layer(xs, c, lp, rope, (cache_ckv[:, l], cache_kpe[:, l]))
    new_ckv = jnp.stack(ckv_list, axis=1)
    new_kpe = jnp.stack(kpe_list, axis=1)
    return (xp, xs, new_ckv, new_kpe)
```

```python
import math
from contextlib import ExitStack

import numpy as np
import ml_dtypes

import concourse.bass as bass
import concourse.mybir as mybir
from concourse.bass_utils import run_bass_kernel_spmd

F32 = mybir.dt.float32
BF16 = mybir.dt.bfloat16
AF = mybir.ActivationFunctionType
ALU = mybir.AluOpType

D = 2048
DC = 16
DEPTH = 2
TS = 2048
LP = 256
T = 2560
PAST = 512
NKEY = T + PAST
D_A = 1024
QL = 512
KVL = 256
ROPE = 64
D_FF = 5632
N_IN = 13120
EPS = 1e-6
ATTN_SCALE = 192 ** -0.5
SEGS = [(0, 2048), (2048, 2304), (2304, 2560)]
TILES = [(0, 512), (512, 512), (1024, 512), (1536, 512), (2048, 512)]
TWO_PI = 2.0 * math.pi

VEC_SPECS = [("norm_mix_pre", 16), ("norm_mix_post", 16), ("norm_ffn_pre", 16), ("norm_ffn_post", 16),
             ("ada_b", 96), ("hy_conv_w0", 24), ("hy_conv_w1", 24), ("hy_conv_w2", 24), ("hy_conv_b", 24),
             ("q_norm", 4), ("kv_norm", 2), ("sc_w0", 8), ("sc_w1", 8), ("sc_w2", 8),
             ("ffn_w0", 88), ("ffn_w1", 88), ("ffn_w2", 88), ("ffn_b", 88)]
VCOL = {}
_c = 0
for _n, _k in VEC_SPECS:
    VCOL[_n] = _c
    _c += _k
NVEC = _c


class Buf:
    def __init__(self, name=""):
        self.name = name
        self.w = None
        self.r = {}


class Q:
    def __init__(self, nc, es, eng, name, is_dma=False, ring=8):
        self.nc = nc
        self.eng = eng
        self.name = name
        self.is_dma = is_dma
        self.known = {}
        if is_dma:
            self.ring = [es.enter_context(nc.semaphore(f"{name}_d{i}")) for i in range(ring)]
            self.ring_cnt = [0] * ring
            self.i = 0
        else:
            self.sem = es.enter_context(nc.semaphore(f"{name}_s"))
            self.n = 0

    def wait(self, tok, hazard="raw"):
        if tok is None:
            return
        s, v = tok
        if (not self.is_dma) and s is self.sem and hazard != "raw":
            return
        k = id(s)
        if self.known.get(k, 0) >= v:
            return
        self.eng.wait_ge(s, v)
        self.known[k] = v

    def acquire(self, reads, writes):
        for b in reads:
            self.wait(b.w, "raw")
        for b in writes:
            for t in list(b.r.values()):
                self.wait(t, "war")
            self.wait(b.w, "waw")

    def release(self, tok, reads, writes):
        for b in reads:
            b.r[id(tok[0])] = tok
        for b in writes:
            b.w = tok
            b.r = {}

    def stamp(self, ins):
        self.n += 1
        ins.then_inc(self.sem, 1)
        return (self.sem, self.n)

    def op(self, ins_fn, reads=(), writes=()):
        self.acquire(reads, writes)
        ins = ins_fn()
        tok = self.stamp(ins)
        self.release(tok, reads, writes)
        return tok

    def dma(self, out, in_, reads=(), writes=()):
        slot = self.i % len(self.ring)
        self.i += 1
        s = self.ring[slot]
        if self.ring_cnt[slot] > 0:
            self.wait((s, 16 * self.ring_cnt[slot]))
        self.acquire(reads, writes)
        self.ring_cnt[slot] += 1
        self.eng.dma_start(out=out, in_=in_).then_inc(s, 16)
        tok = (s, 16 * self.ring_cnt[slot])
        self.release(tok, reads, writes)
        return tok

    def all_toks(self):
        if self.is_dma:
            return [(s, 16 * c) for s, c in zip(self.ring, self.ring_cnt) if c > 0]
        return [(self.sem, self.n)] if self.n > 0 else []


class Ring:
    def __init__(self, items):
        self.items = items
        self.i = 0

    def next(self):
        it = self.items[self.i % len(self.items)]
        self.i += 1
        return it


class MK:
    def __init__(self, debug=False, depth=DEPTH):
        self.debug = debug
        self.depth = depth
        self.nc = bass.Bass("TRN2", target_bir_lowering=False)
        self.din = {}
        self.uid = 0

    def inp(self, name, shape, dt=F32):
        t = self.nc.dram_tensor(name, list(shape), dt, kind="ExternalInput").ap()
        self.din[name] = t
        return t

    def outp(self, name, shape, dt=F32):
        return self.nc.dram_tensor(name, list(shape), dt, kind="ExternalOutput").ap()

    def scratch(self, name, shape, dt):
        kind = "ExternalOutput" if self.debug else "Internal"
        return (self.nc.dram_tensor(name, list(shape), dt, kind=kind).ap(), Buf(name))

    def sb(self, es, name, shape, dt):
        self.uid += 1
        t = es.enter_context(self.nc.sbuf_tensor(f"{name}_{self.uid}", list(shape), dt))
        return t, Buf(name)

    def sbring(self, es, name, shape, dt, n):
        return Ring([self.sb(es, f"{name}{i}", shape, dt) for i in range(n)])

    def barrier(self):
        qs = [self.pe, self.act, self.dve, self.sp, self.pool]
        toks = []
        for q in qs:
            toks += q.all_toks()
        for q in qs:
            for t in toks:
                q.wait(t)

    def mm_group(self, ps_ap, ps_buf, pairs, reads):
        pe = self.pe
        pe.acquire(reads, [ps_buf])
        n = len(pairs)
        ins = None
        for i, (l, r) in enumerate(pairs):
            ins = self.nc.tensor.matmul(ps_ap, l, r, start=(i == 0), stop=(i == n - 1))
        tok = pe.stamp(ins)
        pe.release(tok, reads, [ps_buf])
        return tok

    def A(self, ins_fn, reads=(), writes=()):
        return self.act.op(ins_fn, reads, writes)

    def V(self, ins_fn, reads=(), writes=()):
        return self.dve.op(ins_fn, reads, writes)

    def act_fn(self, out, in_, func, reads, writes, bias=None, scale=None):
        kw = {}
        if bias is not None:
            kw["bias"] = bias
        if scale is not None:
            kw["scale"] = scale
        return self.A(lambda: self.nc.scalar.activation(out=out, in_=in_, func=func, **kw), reads, writes)

    def tt(self, out, a, b, op, reads, writes):
        return self.V(lambda: self.nc.vector.tensor_tensor(out=out, in0=a, in1=b, op=op), reads, writes)

    def ts(self, out, a, s1, s2, op0, op1, reads, writes):
        if op1 is None:
            return self.V(lambda: self.nc.vector.tensor_scalar(out=out, in0=a, scalar1=s1, scalar2=None, op0=op0),
                          reads, writes)
        return self.V(lambda: self.nc.vector.tensor_scalar(out=out, in0=a, scalar1=s1, scalar2=s2, op0=op0, op1=op1),
                      reads, writes)

    def stt(self, out, a, s, b, op0, op1, reads, writes):
        return self.V(lambda: self.nc.vector.scalar_tensor_tensor(out=out, in0=a, scalar=s, in1=b, op0=op0, op1=op1),
                      reads, writes)

    def gemm(self, chunks, tiles, on_tile, on_chunk=None, kcmax=16, cast=True, nps=4, bg=None, banks=None,
             on_end=None):
        nc = self.nc
        with ExitStack() as es:
            wring = self.sbring(es, "wr", [128, kcmax, 128], BF16, 3)
            q = self.pool if cast else self.sp
            loaded = {}

            def load(ci):
                if ci >= len(chunks) or ci in loaded:
                    return
                ch = chunks[ci]
                wt, wb = wring.next()
                off = 0
                for view, w in ch["pieces"]:
                    q.dma(wt[:, 0:ch["kc"], off:off + w], view, reads=ch.get("wreads", ()), writes=[wb])
                    off += w
                loaded[ci] = (wt, wb)

            load(0)
            load(1)
            pi = 0
            if bg is not None:
                bgch, bg_tile, bg_per = bg
                bring = self.sbring(es, "bwr", [128, 16, 128], BF16, 2)
                bloaded = {}
                bstate = {"next": 0, "acc": 0.0}

                def bload(bi):
                    if bi >= len(bgch) or bi in bloaded:
                        return
                    bt, bb = bring.next()
                    self.pool.dma(bt[:, 0:bgch[bi]["kc"], :], bgch[bi]["pieces"][0][0], writes=[bb])
                    bloaded[bi] = (bt, bb)

                def bstep():
                    if bstate.get("pending") is not None:
                        bg_tile[1](*bstate.pop("pending"))
                    bi = bstate["next"]
                    if bi >= len(bgch):
                        return
                    bstate["next"] += 1
                    bload(bi)
                    bload(bi + 1)
                    bt, bb = bloaded.pop(bi)
                    bch = bgch[bi]
                    bat, bab = bch["act"]
                    ps_ap = self.ps[6][0:2, 0:128]
                    self.mm_group(ps_ap, self.psb[6], [(bat[:, k, :], bt[:, k, :]) for k in range(bch["kc"])],
                                  [bb, bab])
                    bstate["pending"] = bg_tile[0](bch, ps_ap, self.psb[6])

                bload(0)
                bload(1)
            for ci, ch in enumerate(chunks):
                load(ci + 2)
                if bg is not None:
                    bstate["acc"] += bg_per
                    while bstate["acc"] >= 1.0:
                        bstate["acc"] -= 1.0
                        bstep()
                wt, wb = loaded.pop(ci)
                at, ab = ch["act"]
                M = ch["M"]
                for ti, (t0, n) in enumerate(tiles):
                    p = banks[pi % len(banks)] if banks is not None else pi % nps
                    pi += 1
                    ps_ap = self.ps[p][0:M, 0:n]
                    pairs = [(wt[:, k, 0:M], at[:, k, t0:t0 + n]) for k in range(ch["kc"])]
                    self.mm_group(ps_ap, self.psb[p], pairs, [wb, ab[ti] if isinstance(ab, list) else ab])
                    on_tile(ci, ch, ti, t0, n, ps_ap, self.psb[p])
                if on_chunk is not None:
                    on_chunk(ci, ch)
            if bg is not None:
                while bstate["next"] < len(bgch):
                    bstep()
                bstep()
            if on_end is not None:
                on_end()
            self.barrier()

    def wview(self, w2d, kc, c0, w):
        return w2d.rearrange("(kc p) n -> p kc n", p=128)[:, 0:kc, c0:c0 + w]

    def rstd_from(self, es, get_chunk, nchunks, nfeat, name):
        nc = self.nc
        rstd, rstd_b = self.sb(es, name, [128, T], F32)
        with ExitStack() as es2:
            sq = self.sbring(es2, "sq", [128, T], BF16, 2)
            for k in range(nchunks):
                ap, b = get_chunk(k)
                st, sbf = sq.next()
                self.act_fn(st[:], ap, AF.Square, [b], [sbf])
                for ti, (t0, n) in enumerate(TILES):
                    pe = self.pe
                    pe.acquire([sbf, self.ones_b], [self.psb[ti]])
                    ins = nc.tensor.matmul(self.ps[ti][:, 0:n], self.ones[:], st[:, t0:t0 + n],
                                           start=(k == 0), stop=(k == nchunks - 1))
                    tok = pe.stamp(ins)
                    pe.release(tok, [sbf, self.ones_b], [self.psb[ti]])
            for ti, (t0, n) in enumerate(TILES):
                self.ts(rstd[:, t0:t0 + n], self.ps[ti][:, 0:n], 1.0 / nfeat, EPS, ALU.mult, ALU.add,
                        [self.psb[ti]], [rstd_b])
            self.act_fn(rstd[:], rstd[:], AF.Ln, [rstd_b], [rstd_b])
            self.act_fn(rstd[:], rstd[:], AF.Exp, [rstd_b], [rstd_b], scale=-0.5)
            self.barrier()
        return rstd, rstd_b

    def norm_mod(self, es, x_src, x_srcb, Atab, Btab):
        nc = self.nc
        hT, hTb = self.sb(es, "hT", [128, DC, T], BF16)
        with ExitStack() as es2:
            xr = self.sbring(es2, "xr", [128, T], F32, 3)

            def getx(k):
                xt, xb = xr.next()
                self.sp.dma(xt[:], x_src[k * 128:(k + 1) * 128, :], reads=[x_srcb], writes=[xb])
                return xt[:], xb

            rstd, rstd_b = self.rstd_from(es2, getx, DC, D, "rstd")
            tmpr = self.sbring(es2, "nt", [128, T], F32, 2)
            for k in range(DC):
                xa, xb = getx(k)
                tt_, tb = tmpr.next()
                self.tt(tt_[:], xa, rstd[:], ALU.mult, [xb, rstd_b], [tb])
                for s, (a, b) in ((0, (0, TS)), (1, (TS, T))):
                    self.act_fn(hT[:, k, a:b], tt_[:, a:b], AF.Identity, [tb, self.modb], [hTb],
                                bias=Btab[:, k, s:s + 1], scale=Atab[:, k, s:s + 1])
            self.barrier()
        return hT, hTb

    def post_norm_res(self, OO, x_src, x_srcb, x_dst, x_dstb, Gtab, hbuf, nxt, between=None, acc_banks=None):
        nc = self.nc
        sp = self.sp
        hb = [Buf(f"hb{k}") for k in range(DC)]
        first = OO is None
        with ExitStack() as es:
            if not first:
                for k in range(DC):
                    sp.dma(hbuf[:, k, :], OO[0][k * 128:(k + 1) * 128, :], reads=[OO[1]], writes=[hb[k]])
                if acc_banks is None:
                    rstd, rstd_b = self.rstd_from(es, lambda k: (hbuf[:, k, :], hb[k]), DC, D, "rstd2")
                else:
                    rstd, rstd_b = self.sb(es, "rstd2", [128, T], F32)
                    for ti, (t0, n) in enumerate(TILES):
                        bk = acc_banks[ti]
                        self.ts(rstd[:, t0:t0 + n], self.ps[bk][:, 0:n], 1.0 / D, EPS, ALU.mult, ALU.add,
                                [self.psb[bk]], [rstd_b])
                    self.act_fn(rstd[:], rstd[:], AF.Ln, [rstd_b], [rstd_b])
                    self.act_fn(rstd[:], rstd[:], AF.Exp, [rstd_b], [rstd_b], scale=-0.5)
            else:
                rstd, rstd_b = self.sb(es, "rstd2", [128, T], F32)
            xr = self.sbring(es, "xr2", [128, T], F32, 3)
            dr = xr if first else self.sbring(es, "dr2", [128, T], F32, 3)
            sq = self.sbring(es, "sq2", [128, T], BF16, 2)
            xl = {}

            def load_x(k):
                if k < DC and k not in xl:
                    xt_, xb_ = xr.next()
                    sp.dma(xt_[:], x_src[k * 128:(k + 1) * 128, :], reads=[x_srcb], writes=[xb_])
                    xl[k] = (xt_, xb_)

            load_x(0)
            load_x(1)
            for k in range(DC):
                load_x(k + 2)
                xt, xb = xl.pop(k)
                if first:
                    dt_, db = xt, xb
                else:
                    dt_, db = dr.next()
                    for s, (a, b) in ((0, (0, TS)), (1, (TS, T))):
                        self.stt(dt_[:, a:b], hbuf[:, k, a:b], Gtab[:, k, s:s + 1], rstd[:, a:b], ALU.mult,
                                 ALU.mult, [hb[k], rstd_b, self.modb], [db])
                    self.tt(dt_[:], dt_[:], xt[:], ALU.add, [db, xb], [db])
                    sp.dma(x_dst[k * 128:(k + 1) * 128, :], dt_[:], reads=[db], writes=[x_dstb])
                if nxt is not None:
                    st, sbf = sq.next()
                    self.act_fn(st[:], dt_[:], AF.Square, [db], [sbf])
                    for ti, (t0, n) in enumerate(TILES):
                        pe = self.pe
                        pe.acquire([sbf, self.ones_b], [self.psb[ti]])
                        ins = nc.tensor.matmul(self.ps[ti][:, 0:n], self.ones[:], st[:, t0:t0 + n],
                                               start=(k == 0), stop=(k == DC - 1))
                        tok = pe.stamp(ins)
                        pe.release(tok, [sbf, self.ones_b], [self.psb[ti]])
                    self.act_fn(hbuf[:, k, :], dt_[:], AF.Copy, [db], [hb[k]])
            if between is not None:
                between()
            if nxt is not None:
                Atab, Btab = nxt
                for ti, (t0, n) in enumerate(TILES):
                    self.ts(rstd[:, t0:t0 + n], self.ps[ti][:, 0:n], 1.0 / D, EPS, ALU.mult, ALU.add,
                            [self.psb[ti]], [rstd_b])
                self.act_fn(rstd[:], rstd[:], AF.Ln, [rstd_b], [rstd_b])
                self.act_fn(rstd[:], rstd[:], AF.Exp, [rstd_b], [rstd_b], scale=-0.5)
                for k in range(DC):
                    dt_, db = dr.next()
                    self.tt(dt_[:], hbuf[:, k, :], rstd[:], ALU.mult, [hb[k], rstd_b], [db])
                    for s, (a, b) in ((0, (0, TS)), (1, (TS, T))):
                        self.act_fn(hbuf[:, k, a:b], dt_[:, a:b], AF.Identity, [db, self.modb], [hb[k]],
                                    bias=Btab[:, k, s:s + 1], scale=Atab[:, k, s:s + 1])
            self.barrier()
        return hbuf, Buf("hT")

    def conv3(self, out, outb, src, srcb, w0, w1, w2, bias):
        if bias is not None:
            self.act_fn(out, src, AF.Identity, [srcb, self.vecb], [outb], bias=bias, scale=w1)
        else:
            self.act_fn(out, src, AF.Identity, [srcb, self.vecb], [outb], scale=w1)
        for (a, b) in SEGS:
            self.stt(out[:, a + 1:b], src[:, a:b - 1], w0, out[:, a + 1:b], ALU.mult, ALU.add,
                     [srcb, outb, self.vecb], [outb])
            self.stt(out[:, a:b - 1], src[:, a + 1:b], w2, out[:, a:b - 1], ALU.mult, ALU.add,
                     [srcb, outb, self.vecb], [outb])

    def vc(self, l, name, k):
        c = VCOL[name] + k
        return self.vecT[:, l, c:c + 1]

    def build(self):
        nc = self.nc
        dbg = self.debug
        xT_in = self.inp("xT_in", [D, T])
        cT_in = self.inp("cT", [128, DC, 2])
        cckvT = self.inp("cckvT", [DEPTH, 128, 2, PAST])
        ckpeT = self.inp("ckpeT", [DEPTH, ROPE, PAST])
        vecT_in = self.inp("vecT", [128, DEPTH, NVEC])
        hyvec_in = self.inp("hyvec", [DEPTH, 64, 4])
        hybias_in = self.inp("hy_bias", [DEPTH, 128, D_A])
        W = {}
        for name, shape in [("ada_w", [DEPTH, D, 6 * D]), ("w_in", [DEPTH, D, N_IN]), ("hy_f_w1", [DEPTH, 33, 64]),
                            ("hy_f_w2", [DEPTH, 64, 64]), ("hy_f_w3", [DEPTH, 64, 2 * D_A]),
                            ("hy_f_b3", [DEPTH, 2 * D_A]), ("w_uq", [DEPTH, QL, 1536]),
                            ("w_ukv", [DEPTH, KVL, 2048]), ("w_br_a", [DEPTH, 1024, D]),
                            ("w_br_b", [DEPTH, 1024, D]), ("w_br_c", [DEPTH, 1024, D]), ("w_o", [DEPTH, D, D]),
                            ("ffn_up", [DEPTH, D, 2 * D_FF]), ("ffn_down", [DEPTH, D_FF, D])]:
            W[name] = self.inp(name, shape)
        C = {}
        for L in (TS, LP):
            for nm in ("Fc", "Fs"):
                C[f"{nm}{L}"] = self.inp(f"{nm}{L}", [L // 128, 128, L // 128, 128], BF16)
            for nm in ("Gc", "Gs"):
                C[f"{nm}{L}"] = self.inp(f"{nm}{L}", [max(1, L // 512), 128, L // 128, min(512, L)], BF16)
            C[f"zT{L}"] = self.inp(f"zT{L}", [33, L])
            C[f"negtn{L}"] = self.inp(f"negtn{L}", [128, L // 128])
        C["delta_b"] = self.inp("delta_b", [128, D_A])
        C["alt"] = self.inp("alt", [128, 1], BF16)
        C["CC"] = self.inp("CC", [ROPE, T])
        C["SS"] = self.inp("SS", [ROPE, T])
        C["ident"] = self.inp("ident", [128, 128], BF16)

        yT = self.outp("yT", [D, T])
        ckv_out = self.outp("ckv_out", [DEPTH, KVL, 2 * LP])
        kpe_out = self.outp("kpe_out", [DEPTH, ROPE, 2 * LP])
        yT_b, ckv_out_b, kpe_out_b = Buf(), Buf(), Buf()

        xa_s = self.scratch("x_a", [D, T], F32)
        xb_s = self.scratch("x_b", [D, T], F32)
        X0 = self.scratch("X0", [D_A, T], BF16)
        ZZ = self.scratch("ZZ", [D_A, T], BF16)
        SIG = self.scratch("SIG", [3 * D, T], BF16)
        YA = self.scratch("YA", [D_A, T], BF16)
        YB = self.scratch("YB", [D_A, T], BF16)
        YC = self.scratch("YC", [D_A, T], BF16)
        MM = self.scratch("MM", [D, T], BF16)
        OO = self.scratch("OO", [D, T], BF16)
        FF = self.scratch("FF", [D_FF, T], BF16)
        RAW = self.scratch("RAW", [1024, T], F32)
        xin_b = Buf("xin")
        cin_b = Buf("cin")

        with ExitStack() as es:
            self.pe = Q(nc, es, nc.tensor, "pe")
            self.act = Q(nc, es, nc.scalar, "act")
            self.dve = Q(nc, es, nc.vector, "dve")
            self.sp = Q(nc, es, nc.sync, "sp", is_dma=True)
            self.pool = Q(nc, es, nc.gpsimd, "pool", is_dma=True)
            sp, pool = self.sp, self.pool
            self.ps = [es.enter_context(nc.psum_tensor(f"ps{i}", [128, 512], F32)) for i in range(7)]
            self.psb = [Buf(f"ps{i}") for i in range(7)]
            self.pst = es.enter_context(nc.psum_tensor("pst", [128, 1024], BF16))
            self.pstb = Buf("pst")
            self.ones, self.ones_b = self.sb(es, "ones", [128, 128], BF16)
            self.V(lambda: nc.vector.memset(self.ones[:], 1.0), [], [self.ones_b])
            self.ident, self.ident_b = self.sb(es, "ident", [128, 128], BF16)
            sp.dma(self.ident[:], C["ident"][:], writes=[self.ident_b])
            self.vecT, self.vecb = self.sb(es, "vecT", [128, DEPTH, NVEC], F32)
            sp.dma(self.vecT[:], vecT_in[:], writes=[self.vecb])
            self.modtab, self.modb = self.sb(es, "modtab", [128, DEPTH, 6, DC, 2], F32)

            cT, cTb = self.sb(es, "cT", [128, DC, 2], F32)
            sp.dma(cT[:], cT_in[:], writes=[cTb])
            scT, scTb = self.sb(es, "scT", [128, DC, 2], BF16)
            self.act_fn(scT[:], cT[:], AF.Silu, [cTb], [scTb])
            modraw, modrawb = self.sb(es, "modraw", [128, DEPTH, 96, 2], F32)

            def ada_chunk(l, m):
                return dict(act=(scT, scTb), kc=DC, M=128, tag=(l, m),
                            pieces=[(self.wview(W["ada_w"][l], DC, m * 128, 128), 128)])

            def ada_tile(ch, ps_ap, psb):
                l_, m = ch["tag"]
                self.ts(modraw[:, l_, m, :], ps_ap, self.vc(l_, "ada_b", m), None, ALU.add, None,
                        [psb, self.vecb], [modrawb])

            ident2, ident2b = self.sb(es, "ident2", [2, 2], F32)
            self.V(lambda: nc.vector.tensor_copy(out=ident2[:], in_=self.ident[0:2, 0:2]), [self.ident_b], [ident2b])
            mrow = self.sbring(es, "mrow", [2, 128], F32, 2)

            def ada_tile_bg_a(ch, ps_row, psb):
                rt, rb = mrow.next()
                self.act_fn(rt[:], ps_row, AF.Copy, [psb], [rb])
                return (ch, rt, rb)

            def ada_tile_bg_b(ch, rt, rb):
                l_, m = ch["tag"]
                self.mm_group(self.ps[5][:, 0:2], self.psb[5], [(rt[:], ident2[:])], [rb, ident2b])
                self.ts(modraw[:, l_, m, :], self.ps[5][:, 0:2], self.vc(l_, "ada_b", m), None, ALU.add, None,
                        [self.psb[5], self.vecb], [modrawb])

            def mod_tables(l_, which):
                mt = self.modtab
                for s_ in range(2):
                    for k in range(DC):
                        for (ai, bi, gi, sh_c, sc_c, g_c, npre, npost) in (
                                (0, 1, 2, 0, 16, 32, "norm_mix_pre", "norm_mix_post"),
                                (3, 4, 5, 48, 64, 80, "norm_ffn_pre", "norm_ffn_post")):
                            if "ab" in which and ai == 0 or "rest" in which and ai == 3:
                                self.ts(mt[:, l_, ai, k, s_:s_ + 1], modraw[:, l_, sc_c + k, s_:s_ + 1], 1.0,
                                        self.vc(l_, npre, k), ALU.add, ALU.mult, [modrawb, self.vecb], [self.modb])
                                self.V(lambda: nc.vector.tensor_copy(out=mt[:, l_, bi, k, s_:s_ + 1],
                                                                     in_=modraw[:, l_, sh_c + k, s_:s_ + 1]),
                                       [modrawb], [self.modb])
                            if "rest" in which:
                                self.ts(mt[:, l_, gi, k, s_:s_ + 1], modraw[:, l_, g_c + k, s_:s_ + 1],
                                        self.vc(l_, npost, k), None, ALU.mult, None, [modrawb, self.vecb],
                                        [self.modb])

            def ada_fg():
                self.gemm([ada_chunk(0, m) for m in range(32)], [(0, 2)],
                          lambda ci, ch, ti, t0, n, ps_ap, psb: ada_tile(ch, ps_ap, psb), banks=[5, 6])
                mod_tables(0, ("ab",))

            bg_chunks = [ada_chunk(0, m) for m in range(32, 96)]
            for l_ in range(1, self.depth):
                bg_chunks += [ada_chunk(l_, m) for m in range(96)]
            self.ada_bg = (bg_chunks[:104], (ada_tile_bg_a, ada_tile_bg_b), 1.0)
            self.ada_bg2 = (bg_chunks[104:], (ada_tile_bg_a, ada_tile_bg_b), 1.0) if len(bg_chunks) > 104 else None
            self.mod_tables = mod_tables

            x_cur, x_curb = xT_in, xin_b
            mt = self.modtab

            def open_layer(l):
                esl = ExitStack()
                cqn, cqnb = self.sb(esl, "cqn", [128, 4, T], BF16)
                keysT, keysTb = self.sb(esl, "keysT", [128, 2, NKEY], BF16)
                kperT, kperTb = self.sb(esl, "kperT", [ROPE, NKEY], BF16)
                pool.dma(keysT[:, :, T:NKEY], cckvT[l], writes=[keysTb])
                pool.dma(kperT[:, T:NKEY], ckpeT[l], writes=[kperTb])
                return esl, (cqn, cqnb, keysT, keysTb, kperT, kperTb)

            esl, att = open_layer(0)
            esh = ExitStack()
            hb0 = self.sb(esh, "hbuf", [128, DC, T], BF16)
            hT, hTb = self.post_norm_res(None, x_cur, x_curb, None, None, None, hb0[0],
                                         (mt[:, 0, 0], mt[:, 0, 1]), between=ada_fg)
            for l in range(self.depth):
                last = (l == self.depth - 1)
                x_mid, x_midb = xa_s
                x_nxt, x_nxtb = (yT, yT_b) if last else xb_s
                G1 = mt[:, l, 2]
                A2, B2, G2 = mt[:, l, 3], mt[:, l, 4], mt[:, l, 5]
                cqn, cqnb, keysT, keysTb, kperT, kperTb = att
                self.inproj(l, esh, hT, hTb, W["w_in"][l], X0, ZZ, SIG, YC, RAW, cqn, cqnb, keysT,
                            keysTb, kperT, kperTb, ckv_out, ckv_out_b, kpe_out, kpe_out_b, C,
                            bg=self.ada_bg if l == 0 else None)
                if l == 0:
                    self.mod_tables(0, ("rest",))
                    self.barrier()
                self.attention(l, W["w_uq"][l], W["w_ukv"][l], cqn, cqnb, keysT, keysTb, kperT, kperTb, YB, C)
                esl.close()
                self.hyena(l, W, C, hyvec_in, hybias_in, X0, ZZ, YA)
                self.merge(l, W, SIG, YA, YB, YC, MM)
                self.outproj(W["w_o"][l], DC, MM, OO, [TILES])
                with ExitStack() as es2:
                    hb2 = self.sb(es2, "hbuf", [128, DC, T], BF16)
                    hT2, hT2b = self.post_norm_res(OO, x_cur, x_curb, x_mid, x_midb, G1, hb2[0], (A2, B2),
                                                   acc_banks=[2, 3, 4, 5, 6])
                    self.ffn_up(es2, l, hT2, hT2b, W["ffn_up"][l], FF, bg=self.ada_bg2 if l == 0 else None)
                    if l == 0:
                        for l_ in range(1, self.depth):
                            self.mod_tables(l_, ("ab", "rest"))
                        self.barrier()
                self.outproj(W["ffn_down"][l], D_FF // 128, FF, OO, [TILES[0:3], TILES[3:5]])
                if not last:
                    esl, att = open_layer(l + 1)
                    esh = ExitStack()
                    hb1 = self.sb(esh, "hbuf", [128, DC, T], BF16)
                    hT, hTb = self.post_norm_res(OO, x_mid, x_midb, x_nxt, x_nxtb, G2, hb1[0],
                                                 (mt[:, l + 1, 0], mt[:, l + 1, 1]), acc_banks=[2, 3, 4, 5, 6])
                else:
                    with ExitStack() as es3:
                        hb3 = self.sb(es3, "hbuf", [128, DC, T], BF16)
                        self.post_norm_res(OO, x_mid, x_midb, x_nxt, x_nxtb, G2, hb3[0], None,
                                           acc_banks=[2, 3, 4, 5, 6])
                x_cur, x_curb = x_nxt, x_nxtb
            self.barrier()
        return nc

    def inproj(self, l, esh, hT, hTb, w_in, X0, ZZ, SIG, YC, RAW, cqn, cqnb, keysT, keysTb, kperT, kperTb,
               ckv_out, ckv_out_b, kpe_out, kpe_out_b, C, bg=None):
        nc = self.nc
        sp = self.sp
        with esh as es:
            act = (hT, hTb)

            def mk(kind, idx, c0, M=128, pieces=None):
                if pieces is None:
                    pieces = [(self.wview(w_in, DC, c0, M), M)]
                return dict(act=act, kc=DC, M=M, tag=(kind, idx), pieces=pieces)

            chunks = []
            for c in range(4):
                chunks.append(mk("raw", c, 3072 + c * 128))
            for c in range(2):
                chunks.append(mk("raw", 4 + c, 3584 + c * 128))
            chunks.append(mk("raw", 6, 3840, M=64))
            chunks.append(mk("raw", 7, 0, M=64, pieces=[(self.wview(w_in, DC, 3872, 32), 32),
                                                         (self.wview(w_in, DC, 3840, 32), 32)]))
            for c in range(8):
                chunks += [mk("x0", c, c * 128), mk("x1", c, 1024 + c * 128), mk("v", c, 2048 + c * 128)]
            for c in range(8):
                chunks += [mk("scb", c, 3904 + c * 128), mk("scc", c, 4928 + c * 128), mk("scu", c, 5952 + c * 128)]
            for c in range(48):
                chunks.append(mk("gate", c, 6976 + c * 128))

            stg = self.sbring(es, "stg", [128, T], F32, 3)
            cvr = self.sbring(es, "cv", [128, T], F32, 2)
            obr = self.sbring(es, "ob", [128, T], BF16, 2)
            st = {}

            def on_tile(ci, ch, ti, t0, n, ps_ap, psb):
                kind, idx = ch["tag"]
                M = ch["M"]
                if kind == "gate":
                    if ti == 0:
                        st["ob"] = obr.next()
                    ot, ob_ = st["ob"]
                    self.act_fn(ot[:, t0:t0 + n], ps_ap, AF.Sigmoid, [psb], [ob_])
                else:
                    if ti == 0:
                        st["stg"] = stg.next()
                    sg, sgb = st["stg"]
                    self.act_fn(sg[0:M, t0:t0 + n], ps_ap, AF.Copy, [psb], [sgb])

            def on_chunk(ci, ch):
                kind, idx = ch["tag"]
                M = ch["M"]
                if kind == "gate":
                    ot, ob_ = st["ob"]
                    sp.dma(SIG[0][idx * 128:(idx + 1) * 128, :], ot[:], reads=[ob_], writes=[SIG[1]])
                elif kind == "raw":
                    sg, sgb = st["stg"]
                    sp.dma(RAW[0][idx * 128:idx * 128 + M, :], sg[0:M, :], reads=[sgb], writes=[RAW[1]])
                elif kind in ("x0", "x1", "v"):
                    sg, sgb = st["stg"]
                    col = {"x0": 0, "x1": 8, "v": 16}[kind] + idx
                    cvt, cvb = cvr.next()
                    self.conv3(cvt[:], cvb, sg[:], sgb, self.vc(l, "hy_conv_w0", col),
                               self.vc(l, "hy_conv_w1", col), self.vc(l, "hy_conv_w2", col),
                               self.vc(l, "hy_conv_b", col))
                    if kind == "x0":
                        ot, ob_ = obr.next()
                        self.act_fn(ot[:], cvt[:], AF.Copy, [cvb], [ob_])
                        sp.dma(X0[0][idx * 128:(idx + 1) * 128, :], ot[:], reads=[ob_], writes=[X0[1]])
                    elif kind == "x1":
                        st["x1"] = (cvt, cvb)
                    else:
                        x1t, x1b = st["x1"]
                        ot, ob_ = obr.next()
                        self.tt(ot[:], x1t[:], cvt[:], ALU.mult, [x1b, cvb], [ob_])
                        sp.dma(ZZ[0][idx * 128:(idx + 1) * 128, :], ot[:], reads=[ob_], writes=[ZZ[1]])
                elif kind == "scb":
                    st["scb"] = st["stg"]
                elif kind == "scc":
                    st["scc"] = st["stg"]
                elif kind == "scu":
                    sg, sgb = st["stg"]
                    cgt, cgb = st["scc"]
                    bt, bb = st["scb"]
                    self.tt(sg[:], sg[:], cgt[:], ALU.mult, [sgb, cgb], [sgb])
                    cvt, cvb = cvr.next()
                    self.conv3(cvt[:], cvb, sg[:], sgb, self.vc(l, "sc_w0", idx), self.vc(l, "sc_w1", idx),
                               self.vc(l, "sc_w2", idx), None)
                    ot, ob_ = obr.next()
                    self.tt(ot[:], cvt[:], bt[:], ALU.mult, [cvb, bb], [ob_])
                    sp.dma(YC[0][idx * 128:(idx + 1) * 128, :], ot[:], reads=[ob_], writes=[YC[1]])

            self.gemm(chunks, TILES, on_tile, on_chunk, bg=bg)
        with ExitStack() as es2:
            cq_raw, cq_rawb = self.sb(es2, "cq_raw", [128, 4, T], F32)
            ckv_raw, ckv_rawb = self.sb(es2, "ckv_raw", [128, 2, T], F32)
            kpe_raw, kpe_rawb = self.sb(es2, "kpe_raw", [ROPE, T], F32)
            ksw_raw, ksw_rawb = self.sb(es2, "ksw_raw", [ROPE, T], F32)
            cc, ccb = self.sb(es2, "cc", [ROPE, T], F32)
            ss, ssb = self.sb(es2, "ss", [ROPE, T], F32)
            sp.dma(cq_raw[:], RAW[0][0:512, :].rearrange("(c p) t -> p c t", p=128), reads=[RAW[1]], writes=[cq_rawb])
            sp.dma(ckv_raw[:], RAW[0][512:768, :].rearrange("(c p) t -> p c t", p=128), reads=[RAW[1]],
                   writes=[ckv_rawb])
            sp.dma(kpe_raw[:], RAW[0][768:832, :], reads=[RAW[1]], writes=[kpe_rawb])
            sp.dma(ksw_raw[:], RAW[0][896:960, :], reads=[RAW[1]], writes=[ksw_rawb])
            sp.dma(cc[:], C["CC"][:], writes=[ccb])
            sp.dma(ss[:], C["SS"][:], writes=[ssb])
            sp.dma(kpe_out[l], kpe_raw[:, TS:T], reads=[kpe_rawb], writes=[kpe_out_b])
            self.tt(ksw_raw[:], ksw_raw[:], ss[:], ALU.mult, [ksw_rawb, ssb], [ksw_rawb])
            self.tt(cc[:], kpe_raw[:], cc[:], ALU.mult, [kpe_rawb, ccb], [ccb])
            self.tt(kperT[:, 0:T], cc[:], ksw_raw[:], ALU.add, [ccb, ksw_rawb], [kperTb])
            rq, rqb = self.rstd_from(es2, lambda k: (cq_raw[:, k, :], cq_rawb), 4, QL, "rq")
            for k in range(4):
                self.stt(cqn[:, k, :], cq_raw[:, k, :], self.vc(l, "q_norm", k), rq[:], ALU.mult, ALU.mult,
                         [cq_rawb, rqb, self.vecb], [cqnb])
            rk, rkb = self.rstd_from(es2, lambda k: (ckv_raw[:, k, :], ckv_rawb), 2, KVL, "rk")
            for k in range(2):
                self.stt(ckv_raw[:, k, :], ckv_raw[:, k, :], self.vc(l, "kv_norm", k), rk[:], ALU.mult, ALU.mult,
                         [ckv_rawb, rkb, self.vecb], [ckv_rawb])
                self.act_fn(keysT[:, k, 0:T], ckv_raw[:, k, :], AF.Copy, [ckv_rawb], [keysTb])
                sp.dma(ckv_out[l, k * 128:(k + 1) * 128, :], ckv_raw[:, k, TS:T], reads=[ckv_rawb],
                       writes=[ckv_out_b])
            self.barrier()

    def attention(self, l, w_uq, w_ukv, cqn, cqnb, keysT, keysTb, kperT, kperTb, YB, C):
        nc = self.nc
        sp, pool = self.sp, self.pool
        with ExitStack() as es:
            wuq, wuqb = self.sb(es, "wuq", [128, 4, 1536], BF16)
            pool.dma(wuq[:], w_uq.rearrange("(kc p) n -> p kc n", p=128), writes=[wuqb])
            wsw, wswb = self.sb(es, "wsw", [128, 4, 8, 64], BF16)
            v4 = w_uq.rearrange("(kc p) (h d) -> p kc h d", p=128, d=192)
            for kc in range(4):
                pool.dma(wsw[:, kc, :, 0:32], v4[:, kc, :, 160:192], writes=[wswb])
                pool.dma(wsw[:, kc, :, 32:64], v4[:, kc, :, 128:160], writes=[wswb])
            wkv, wkvb = self.sb(es, "wkv", [128, 2, 2048], BF16)
            pool.dma(wkv[:], w_ukv.rearrange("(kc p) n -> p kc n", p=128), writes=[wkvb])
            cc, ccb = self.sb(es, "cc2", [ROPE, T], F32)
            ss, ssb = self.sb(es, "ss2", [ROPE, T], F32)
            sp.dma(cc[:], C["CC"][:], writes=[ccb])
            sp.dma(ss[:], C["SS"][:], writes=[ssb])
            qn, qnb = self.sb(es, "qn", [128, T], BF16)
            qp, qpb = self.sb(es, "qp", [ROPE, T], BF16)
            kh, khb = self.sb(es, "kh", [128, NKEY], BF16)
            vh, vhb = self.sb(es, "vh", [128, NKEY // 128, 128], BF16)
            t1, t1b = self.sb(es, "t1", [ROPE, 512], F32)
            t2, t2b = self.sb(es, "t2", [ROPE, 512], F32)
            ptr = self.sbring(es, "pt", [128, 512], BF16, 4)
            rden, rdenb = self.sb(es, "rden", [128, 512], F32)
            ybr = self.sbring(es, "yb", [128, 512], BF16, 3)

            ps, psb = self.ps, self.psb
            sring = Ring([0, 1, 2])
            oi = 0
            for h in range(8):
                for (t0, n) in TILES:
                    p = sring.next()
                    self.mm_group(ps[p][:, 0:n], psb[p],
                                  [(wuq[:, k, h * 192:h * 192 + 128], cqn[:, k, t0:t0 + n]) for k in range(4)],
                                  [wuqb, cqnb])
                    self.act_fn(qn[:, t0:t0 + n], ps[p][:, 0:n], AF.Copy, [psb[p]], [qnb])
                    p = sring.next()
                    self.mm_group(ps[p][0:ROPE, 0:n], psb[p],
                                  [(wuq[:, k, h * 192 + 128:h * 192 + 192], cqn[:, k, t0:t0 + n]) for k in range(4)],
                                  [wuqb, cqnb])
                    self.tt(t1[:, 0:n], ps[p][0:ROPE, 0:n], cc[:, t0:t0 + n], ALU.mult, [psb[p], ccb], [t1b])
                    p = sring.next()
                    self.mm_group(ps[p][0:ROPE, 0:n], psb[p],
                                  [(wsw[:, k, h, :], cqn[:, k, t0:t0 + n]) for k in range(4)], [wswb, cqnb])
                    self.tt(t2[:, 0:n], ps[p][0:ROPE, 0:n], ss[:, t0:t0 + n], ALU.mult, [psb[p], ssb], [t2b])
                    self.tt(qp[:, t0:t0 + n], t1[:, 0:n], t2[:, 0:n], ALU.add, [t1b, t2b], [qpb])
                for kt in range(NKEY // 512):
                    p = sring.next()
                    self.mm_group(ps[p][:, :], psb[p],
                                  [(wkv[:, k, h * 256:h * 256 + 128], keysT[:, k, kt * 512:(kt + 1) * 512])
                                   for k in range(2)], [wkvb, keysTb])
                    self.act_fn(kh[:, kt * 512:(kt + 1) * 512], ps[p][:, :], AF.Copy, [psb[p]], [khb])
                for g in range(NKEY // 512):
                    p = sring.next()
                    for j in range(4):
                        kt = g * 4 + j
                        self.mm_group(ps[p][:, j * 128:(j + 1) * 128], psb[p],
                                      [(keysT[:, k, kt * 128:(kt + 1) * 128], wkv[:, k, h * 256 + 128:h * 256 + 256])
                                       for k in range(2)], [wkvb, keysTb])
                    self.V(lambda: nc.vector.tensor_copy(
                        out=vh[:, g * 4:(g + 1) * 4, :], in_=ps[p][:, :].rearrange("p (j v) -> p j v", v=128)),
                        [psb[p]], [vhb])
                qtiles = [(0, 512, list(range(0, 16)) + list(range(20, 24))),
                          (512, 512, list(range(0, 16)) + list(range(20, 24))),
                          (1024, 512, list(range(0, 16)) + list(range(20, 24))),
                          (1536, 512, list(range(0, 16)) + list(range(20, 24))),
                          (2048, 256, [16, 17]), (2304, 256, [18, 19])]
                for (q0, nq, kts) in qtiles:
                    po = 3 + (oi % 2)
                    pd = 5 + (oi % 2)
                    oi += 1

                    def qk(kt):
                        p = sring.next()
                        self.mm_group(ps[p][:, 0:nq], psb[p],
                                      [(kh[:, kt * 128:(kt + 1) * 128], qn[:, q0:q0 + nq]),
                                       (kperT[:, kt * 128:(kt + 1) * 128], qp[:, q0:q0 + nq])],
                                      [khb, qnb, kperTb, qpb])
                        return p

                    pcur = qk(kts[0])
                    nk = len(kts)
                    self.pe.acquire([], [psb[po], psb[pd]])
                    for i, kt in enumerate(kts):
                        pnext = qk(kts[i + 1]) if i + 1 < nk else None
                        pt, ptb = ptr.next()
                        self.act_fn(pt[:, 0:nq], ps[pcur][:, 0:nq], AF.Exp, [psb[pcur]], [ptb], scale=ATTN_SCALE)
                        self.pe.acquire([ptb, vhb, self.ones_b], [])
                        nc.tensor.matmul(ps[po][:, 0:nq], vh[:, kt, :], pt[:, 0:nq], start=(i == 0),
                                         stop=(i == nk - 1))
                        ins = nc.tensor.matmul(ps[pd][:, 0:nq], self.ones[:], pt[:, 0:nq], start=(i == 0),
                                               stop=(i == nk - 1))
                        tok = self.pe.stamp(ins)
                        self.pe.release(tok, [ptb, vhb, self.ones_b], [psb[po], psb[pd]] if i == nk - 1 else [])
                        pcur = pnext
                    self.V(lambda: nc.vector.reciprocal(out=rden[:, 0:nq], in_=ps[pd][:, 0:nq]), [psb[pd]], [rdenb])
                    ybt, ybb = ybr.next()
                    self.tt(ybt[:, 0:nq], ps[po][:, 0:nq], rden[:, 0:nq], ALU.mult, [psb[po], rdenb], [ybb])
                    sp.dma(YB[0][h * 128:(h + 1) * 128, q0:q0 + nq], ybt[:, 0:nq], reads=[ybb], writes=[YB[1]])
            self.barrier()

    def hyena(self, l, W, C, hyvec_in, hybias_in, X0, ZZ, YA):
        nc = self.nc
        sp, pool = self.sp, self.pool
        ps, psb = self.ps, self.psb
        with ExitStack() as es:
            w1, w1b = self.sb(es, "hw1", [33, 64], F32)
            w2, w2b = self.sb(es, "hw2", [64, 64], F32)
            w3, w3b = self.sb(es, "hw3", [65, 2 * D_A], BF16)
            hv, hvb = self.sb(es, "hv", [64, 4], F32)
            hb, hbb = self.sb(es, "hb", [128, D_A], F32)
            delta, deltab = self.sb(es, "delta", [128, D_A], F32)
            alt, altb = self.sb(es, "alt", [128, 1], BF16)
            sp.dma(w1[:], W["hy_f_w1"][l], writes=[w1b])
            sp.dma(w2[:], W["hy_f_w2"][l], writes=[w2b])
            pool.dma(w3[0:64, :], W["hy_f_w3"][l], writes=[w3b])
            pool.dma(w3[64:65, :], W["hy_f_b3"][l:l + 1, :], writes=[w3b])
            sp.dma(hv[:], hyvec_in[l], writes=[hvb])
            sp.dma(hb[:], hybias_in[l], writes=[hbb])
            sp.dma(delta[:], C["delta_b"][:], writes=[deltab])
            sp.dma(alt[:], C["alt"][:], writes=[altb])
            fb, fbb = self.sb(es, "fb", [64, 2], F32)
            for i in range(2):
                self.ts(fb[:, i:i + 1], hv[:, i:i + 1], hv[:, 2 + i:3 + i], None, ALU.mult, None, [hvb], [fbb])
            for (L, seqs) in ((TS, [0]), (LP, [TS, TS + LP])):
                LC = L // 128
                with ExitStack() as esL:
                    Kre, Kreb = self.sb(esL, "Kre", [128, LC, D_A], BF16)
                    Kim, Kimb = self.sb(esL, "Kim", [128, LC, D_A], BF16)
                    self.hy_filter(l, L, LC, C, w1, w1b, w2, w2b, w3, w3b, hv, hvb, fb, fbb, hb, hbb, delta, deltab,
                                   alt, altb, Kre, Kreb, Kim, Kimb)
                    if L == TS:
                        for half in range(2):
                            self.hy_conv(L, LC, C, [(seqs[0], half * 512)], Kre, Kreb, Kim, Kimb, X0, ZZ, YA)
                    else:
                        self.hy_conv(L, LC, C, [(s0, half * 512) for s0 in seqs for half in range(2)], Kre, Kreb,
                                     Kim, Kimb, X0, ZZ, YA)
            self.barrier()

    def hy_filter(self, l, L, LC, C, w1, w1b, w2, w2b, w3, w3b, hv, hvb, fb, fbb, hb, hbb, delta, deltab, alt, altb,
                  Kre, Kreb, Kim, Kimb):
        nc = self.nc
        sp = self.sp
        ps, psb = self.ps, self.psb
        with ExitStack() as es:
            ntn, ntnb = self.sb(es, "ntn", [128, LC], F32)
            sp.dma(ntn[:], C[f"negtn{L}"][:], writes=[ntnb])
            h2, h2b = self.sb(es, "h2", [65, L], BF16)
            self.V(lambda: nc.vector.memset(h2[64:65, :], 1.0), [], [h2b])
            with ExitStack() as esz:
                zT, zTb = self.sb(esz, "zT", [33, L], F32)
                sp.dma(zT[:], C[f"zT{L}"][:], writes=[zTb])
                h1, h1b = self.sb(esz, "h1", [64, L], F32)
                ya, yab = self.sb(esz, "ysin", [64, 512], F32)
                yk, ykb = self.sb(esz, "ysink", [64, 512], mybir.dt.int32)
                yf, yfb = self.sb(esz, "ysinf", [64, 512], F32)
                tl = [(t0, min(512, L - t0)) for t0 in range(0, L, 512)]
                for (wt, wb, src, srcb, dst, dstb, i) in ((w1, w1b, zT, zTb, h1, h1b, 0),
                                                          (w2, w2b, h1, h1b, h2, h2b, 1)):
                    for (t0, n) in tl:
                        self.mm_group(ps[0][0:64, 0:n], psb[0], [(wt[:], src[:, t0:t0 + n])], [wb, srcb])
                        self.ts(ya[:, 0:n], ps[0][0:64, 0:n], hv[:, 2 + i:3 + i], fb[:, i:i + 1], ALU.mult, ALU.add,
                                [psb[0], hvb, fbb], [yab])
                        self.ts(yk[:, 0:n], ya[:, 0:n], 1.0 / TWO_PI, None, ALU.mult, None, [yab], [ykb])
                        self.V(lambda: nc.vector.tensor_copy(out=yf[:, 0:n], in_=yk[:, 0:n]), [ykb], [yfb])
                        self.stt(ya[:, 0:n], yf[:, 0:n], -TWO_PI, ya[:, 0:n], ALU.mult, ALU.add, [yfb, yab], [yab])
                        self.ts(ya[:, 0:n], ya[:, 0:n], math.pi - 1e-5, -math.pi + 1e-5, ALU.min, ALU.max,
                                [yab], [yab])
                        self.act_fn(dst[0:64, t0:t0 + n], ya[:, 0:n], AF.Sin, [yab], [dstb])
                self.barrier()
            hsT, hsTb = self.sb(es, "hsT", [128, LC, D_A], BF16)
            hdT, hdTb = self.sb(es, "hdT", [128, LC, D_A], BF16)
            winr = self.sbring(es, "win", [128, D_A], F32, 2)
            ab2, ab2b = self.sb(es, "ab2", [128, D_A], F32)
            hf, hfb = self.sb(es, "hf", [128, D_A], F32)
            hbw, hbwb = self.sb(es, "hbw", [128, D_A], F32)
            ab1, ab1b = self.sb(es, "ab1", [128, D_A], F32)
            ab, abb = self.sb(es, "ab", [128, D_A], BF16)
            for i in range(LC):
                for j in range(4):
                    self.mm_group(ps[j][:, :], psb[j], [(h2[:, i * 128:(i + 1) * 128], w3[:, j * 512:(j + 1) * 512])],
                                  [h2b, w3b])
                win, winb = winr.next()
                self.act_fn(win[:], delta[:], AF.Exp, [deltab, ntnb], [winb], scale=ntn[:, i:i + 1])
                for j in range(2):
                    self.tt(hf[:, j * 512:(j + 1) * 512], ps[j][:, :], win[:, j * 512:(j + 1) * 512], ALU.mult,
                            [psb[j], winb], [hfb])
                    self.tt(hbw[:, j * 512:(j + 1) * 512], ps[2 + j][:, :], win[:, j * 512:(j + 1) * 512], ALU.mult,
                            [psb[2 + j], winb], [hbwb])
                if i == 0:
                    self.V(lambda: nc.vector.memset(hbw[0:1, :], 0.0), [], [hbwb])
                self.tt(hsT[:, i, :], hf[:], hbw[:], ALU.add, [hfb, hbwb], [hsTb])
                self.tt(hdT[:, i, :], hf[:], hbw[:], ALU.subtract, [hfb, hbwb], [hdTb])
                self.act_fn(ab1[:], hbw[:], AF.Abs, [hbwb], [ab1b])
                self.act_fn(ab2[:], hf[:], AF.Abs, [hfb], [ab2b])
                self.tt(ab[:], ab1[:], ab2[:], ALU.add, [ab1b, ab2b], [abb])
                for j in range(2):
                    pe = self.pe
                    pe.acquire([abb, self.ones_b], [psb[4 + j]])
                    ins = nc.tensor.matmul(ps[4 + j][:, :], self.ones[:], ab[:, j * 512:(j + 1) * 512],
                                           start=(i == 0), stop=(i == LC - 1))
                    tok = pe.stamp(ins)
                    pe.release(tok, [abb, self.ones_b], [psb[4 + j]])
            rn, rnb = self.sb(es, "rn", [128, D_A], F32)
            for j in range(2):
                self.V(lambda: nc.vector.reciprocal(out=rn[:, j * 512:(j + 1) * 512], in_=ps[4 + j][:, :]),
                       [psb[4 + j]], [rnb])
            ny, nyb = self.sb(es, "ny", [1, D_A], F32)
            for j in range(2):
                self.mm_group(ps[6][0:1, :], psb[6],
                              [(alt[:, 0:1], hsT[:, i, j * 512:(j + 1) * 512]) for i in range(LC)], [altb, hsTb])
                self.tt(ny[:, j * 512:(j + 1) * 512], ps[6][0:1, :], rn[0:1, j * 512:(j + 1) * 512], ALU.mult,
                        [psb[6], rnb], [nyb])
            self.tt(ny[:], ny[:], hb[0:1, :], ALU.add, [nyb, hbb], [nyb])
            self.barrier()
            tmp, tmpb = self.sb(es, "ktmp", [128, 512], F32)
            halves = [(0, 512), (512, 512)]
            for (tab, src, srcb, dst, dstb, addb) in (("Fc", hsT, hsTb, Kre, Kreb, True),
                                                      ("Fs", hdT, hdTb, Kim, Kimb, False)):
                chunks = [dict(act=(src, srcb), kc=LC, M=128, tag=f,
                               pieces=[(C[f"{tab}{L}"][f], 128)]) for f in range(LC)]

                def on_tile(ci, ch, ti, t0, n, ps_ap, psb_, dst=dst, dstb=dstb, addb=addb):
                    f = ch["tag"]
                    if addb:
                        self.tt(tmp[:, 0:n], ps_ap, rn[:, t0:t0 + n], ALU.mult, [psb_, rnb], [tmpb])
                        self.tt(dst[:, f, t0:t0 + n], tmp[:, 0:n], hb[:, t0:t0 + n], ALU.add, [tmpb, hbb], [dstb])
                    else:
                        self.tt(dst[:, f, t0:t0 + n], ps_ap, rn[:, t0:t0 + n], ALU.mult, [psb_, rnb], [dstb])

                self.gemm(chunks, halves, on_tile, kcmax=LC, cast=False)
            self.V(lambda: nc.vector.tensor_copy(out=Kim[0:1, 0, :], in_=ny[:]), [nyb], [Kimb])
            self.barrier()

    def hy_conv(self, L, LC, C, blocks, Kre, Kreb, Kim, Kimb, X0, ZZ, YA):
        nc = self.nc
        sp = self.sp
        ps, psb = self.ps, self.psb
        NB = len(blocks)
        with ExitStack() as es:
            Yre, Yreb = self.sb(es, "Yre", [128, LC, NB * 512], BF16)
            Yim, Yimb = self.sb(es, "Yim", [128, LC, NB * 512], BF16)
            with ExitStack() as es2:
                zzT, zzTb = self.sb(es2, "zzT", [128, LC, NB * 512], BF16)
                zzr = self.sbring(es2, "zz", [128, 4, L], BF16, 2)
                for bi, (s0, c0) in enumerate(blocks):
                    zz, zzb = zzr.next()
                    sp.dma(zz[:], ZZ[0][c0:c0 + 512, s0:s0 + L].rearrange("(c p) t -> p c t", p=128), reads=[ZZ[1]],
                           writes=[zzb])
                    for i in range(LC):
                        self.pe.acquire([zzb, self.ident_b], [self.pstb])
                        ins = None
                        for c in range(4):
                            ins = nc.tensor.transpose(self.pst[:, c * 128:(c + 1) * 128],
                                                      zz[:, c, i * 128:(i + 1) * 128], self.ident[:])
                        tok = self.pe.stamp(ins)
                        self.pe.release(tok, [zzb, self.ident_b], [self.pstb])
                        self.act_fn(zzT[:, i, bi * 512:(bi + 1) * 512], self.pst[:, 0:512], AF.Copy, [self.pstb],
                                    [zzTb])
                zre, zreb = self.sb(es2, "zre", [128, NB * 512], F32)
                zimr = self.sbring(es2, "zim", [128, 512], F32, 2)
                ta, tab_ = self.sb(es2, "ta", [128, 512], F32)
                tb, tbb = self.sb(es2, "tb", [128, 512], F32)
                chunks = []
                for f in range(LC):
                    for nm in ("Fc", "Fs"):
                        chunks.append(dict(act=(zzT, zzTb), kc=LC, M=128, tag=(nm, f),
                                           pieces=[(C[f"{nm}{L}"][f], 128)]))

                def on_tile(ci, ch, ti, t0, n, ps_ap, psb_):
                    nm, f = ch["tag"]
                    c0 = blocks[ti][1]
                    zr = zre[:, t0:t0 + 512]
                    if nm == "Fc":
                        self.act_fn(zr, ps_ap, AF.Copy, [psb_], [zreb])
                    else:
                        zim, zimb = zimr.next()
                        self.act_fn(zim[:], ps_ap, AF.Copy, [psb_], [zimb])
                        kr = Kre[:, f, c0:c0 + 512]
                        ki = Kim[:, f, c0:c0 + 512]
                        yre = Yre[:, f, t0:t0 + 512]
                        yim = Yim[:, f, t0:t0 + 512]
                        self.tt(ta[:], zr, kr, ALU.mult, [zreb, Kreb], [tab_])
                        self.tt(tb[:], zim[:], ki, ALU.mult, [zimb, Kimb], [tbb])
                        self.tt(yre, ta[:], tb[:], ALU.subtract, [tab_, tbb], [Yreb])
                        if f == 0:
                            self.V(lambda: nc.vector.tensor_copy(out=yre[0:1, :], in_=ta[0:1, :]), [tab_], [Yreb])
                        self.tt(ta[:], zr, ki, ALU.mult, [zreb, Kimb], [tab_])
                        if f == 0:
                            self.V(lambda: nc.vector.tensor_copy(out=zr[0:1, :], in_=tb[0:1, :]), [tbb], [zreb])
                        self.tt(tb[:], zim[:], kr, ALU.mult, [zimb, Kreb], [tbb])
                        self.tt(yim, ta[:], tb[:], ALU.add, [tab_, tbb], [Yimb])
                        if f == 0:
                            self.V(lambda: nc.vector.tensor_copy(out=yim[0:1, :], in_=zr[0:1, :]), [zreb], [Yimb])

                self.gemm(chunks, [(bi * 512, 512) for bi in range(NB)], on_tile, kcmax=LC, cast=False)
            with ExitStack() as es3:
                gcr = self.sbring(es3, "gc", [128, LC, 512], BF16, 2)
                gsr = self.sbring(es3, "gs", [128, LC, 512], BF16, 2)
                x0r = self.sbring(es3, "x0", [128, 4, 512], BF16, 3)
                yor = self.sbring(es3, "yo", [128, 512], BF16, 3)
                pr = Ring([0, 1, 2, 3])
                ttiles = [(t0, min(512, L - t0)) for t0 in range(0, L, 512)]
                gl = {}
                xl = {}
                work = [(ti, bi) for ti in range(len(ttiles)) for bi in range(NB)]

                def load_g(ti):
                    if ti < len(ttiles) and ti not in gl:
                        t0, n = ttiles[ti]
                        gct, gcb = gcr.next()
                        gst, gsb = gsr.next()
                        sp.dma(gct[:, :, 0:n], C[f"Gc{L}"][t0 // 512], writes=[gcb])
                        sp.dma(gst[:, :, 0:n], C[f"Gs{L}"][t0 // 512], writes=[gsb])
                        gl[ti] = (gct, gcb, gst, gsb)

                def load_x0(wi):
                    if wi < len(work) and wi not in xl:
                        ti, bi = work[wi]
                        t0, n = ttiles[ti]
                        s0, c0 = blocks[bi]
                        x0t, x0b = x0r.next()
                        sp.dma(x0t[:, :, 0:n],
                               X0[0][c0:c0 + 512, s0 + t0:s0 + t0 + n].rearrange("(c p) t -> p c t", p=128),
                               reads=[X0[1]], writes=[x0b])
                        xl[wi] = (x0t, x0b)

                load_g(0)
                load_x0(0)
                for wi, (ti, bi) in enumerate(work):
                    t0, n = ttiles[ti]
                    s0, c0 = blocks[bi]
                    if bi == 0:
                        load_g(ti + 1)
                    load_x0(wi + 1)
                    gct, gcb, gst, gsb = gl[ti]
                    x0t, x0b = xl.pop(wi)
                    for c in range(4):
                        p = pr.next()
                        cs = bi * 512 + c * 128
                        pairs = [(Yre[:, f, cs:cs + 128], gct[:, f, 0:n]) for f in range(LC)]
                        pairs += [(Yim[:, f, cs:cs + 128], gst[:, f, 0:n]) for f in range(LC)]
                        self.mm_group(ps[p][:, 0:n], psb[p], pairs, [Yreb, Yimb, gcb, gsb])
                        yt, ytb = yor.next()
                        self.tt(yt[:, 0:n], ps[p][:, 0:n], x0t[:, c, 0:n], ALU.mult, [psb[p], x0b], [ytb])
                        sp.dma(YA[0][c0 + c * 128:c0 + (c + 1) * 128, s0 + t0:s0 + t0 + n], yt[:, 0:n], reads=[ytb],
                               writes=[YA[1]])
                self.barrier()

    def merge(self, l, W, SIG, YA, YB, YC, MM):
        nc = self.nc
        sp = self.sp
        with ExitStack() as es:
            ys = []
            for (nm, scr) in (("a", YA), ("b", YB), ("c", YC)):
                yt, ytb = self.sb(es, f"y{nm}", [128, 8, T], BF16)
                sp.dma(yt[:], scr[0].rearrange("(c p) t -> p c t", p=128), reads=[scr[1]], writes=[ytb])
                ys.append((yt, ytb))
            sgr = self.sbring(es, "sg", [128, T], BF16, 4)
            accr = self.sbring(es, "acc", [128, T], F32, 2)
            tmp, tmpb = self.sb(es, "mtmp", [128, 512], F32)
            mor = self.sbring(es, "mo", [128, T], BF16, 2)
            chunks = []
            for j in range(DC):
                for bi, nm in enumerate(("w_br_a", "w_br_b", "w_br_c")):
                    chunks.append(dict(act=ys[bi], kc=8, M=128, tag=(j, bi),
                                       pieces=[(self.wview(W[nm][l], 8, j * 128, 128), 128)]))
            st = {}

            sgl = {}

            def load_sig(ci_):
                if ci_ < len(chunks) and ci_ not in sgl:
                    j_, bi_ = chunks[ci_]["tag"]
                    sgt_, sgb_ = sgr.next()
                    sp.dma(sgt_[:], SIG[0][(bi_ * DC + j_) * 128:(bi_ * DC + j_ + 1) * 128, :], reads=[SIG[1]],
                           writes=[sgb_])
                    sgl[ci_] = (sgt_, sgb_)

            def on_tile(ci, ch, ti, t0, n, ps_ap, psb_):
                j, bi = ch["tag"]
                if ti == 0:
                    load_sig(ci)
                    load_sig(ci + 1)
                    load_sig(ci + 2)
                    st["sg"] = sgl.pop(ci)
                    if bi == 0:
                        st["acc"] = accr.next()
                sgt, sgb = st["sg"]
                at, ab = st["acc"]
                if bi == 0:
                    self.tt(at[:, t0:t0 + n], ps_ap, sgt[:, t0:t0 + n], ALU.mult, [psb_, sgb], [ab])
                else:
                    self.tt(tmp[:, 0:n], ps_ap, sgt[:, t0:t0 + n], ALU.mult, [psb_, sgb], [tmpb])
                    self.tt(at[:, t0:t0 + n], at[:, t0:t0 + n], tmp[:, 0:n], ALU.add, [ab, tmpb], [ab])

            def on_chunk(ci, ch):
                j, bi = ch["tag"]
                if bi == 2:
                    at, ab = st["acc"]
                    mt, mb = mor.next()
                    self.act_fn(mt[:], at[:], AF.Copy, [ab], [mb])
                    sp.dma(MM[0][j * 128:(j + 1) * 128, :], mt[:], reads=[mb], writes=[MM[1]])

            self.gemm(chunks, TILES, on_tile, on_chunk, kcmax=8)

    def outproj(self, w2d, KC, SRC, OO, tile_groups):
        nc = self.nc
        sp = self.sp
        for tg in tile_groups:
            a0 = tg[0][0]
            a1 = tg[-1][0] + tg[-1][1]
            with ExitStack() as es:
                at, _ = self.sb(es, "opa", [128, KC, a1 - a0], BF16)
                ab = [Buf(f"opa{i}") for i in range(len(tg))]
                for i, (t0_, n_) in enumerate(tg):
                    sp.dma(at[:, :, t0_ - a0:t0_ - a0 + n_],
                           SRC[0][:, t0_:t0_ + n_].rearrange("(c p) t -> p c t", p=128), reads=[SRC[1]],
                           writes=[ab[i]])
                otr = self.sbring(es, "opo", [128, 512], BF16, 4)
                sqr = self.sbring(es, "opsq", [128, 512], BF16, 4)
                chunks = [dict(act=(at, ab), kc=KC, M=128, tag=j,
                               pieces=[(self.wview(w2d, KC, j * 128, 128), 128)]) for j in range(DC)]
                tiles = [(t0 - a0, n) for (t0, n) in tg]
                pending = []

                def emit_acc():
                    j, gti, sqt, sqb, n = pending.pop(0)
                    pe = self.pe
                    bank = 2 + gti
                    pe.acquire([sqb, self.ones_b], [self.psb[bank]])
                    ins = nc.tensor.matmul(self.ps[bank][:, 0:n], self.ones[:], sqt[:, 0:n], start=(j == 0),
                                           stop=(j == DC - 1))
                    tok = pe.stamp(ins)
                    pe.release(tok, [sqb, self.ones_b], [self.psb[bank]])

                def on_tile(ci, ch, ti, t0, n, ps_ap, psb_, a0=a0):
                    j = ch["tag"]
                    while len(pending) >= 2:
                        emit_acc()
                    ot, ob = otr.next()
                    self.act_fn(ot[:, 0:n], ps_ap, AF.Copy, [psb_], [ob])
                    sp.dma(OO[0][j * 128:(j + 1) * 128, a0 + t0:a0 + t0 + n], ot[:, 0:n], reads=[ob],
                           writes=[OO[1]])
                    sqt, sqb = sqr.next()
                    self.act_fn(sqt[:, 0:n], ps_ap, AF.Square, [psb_], [sqb])
                    pending.append((j, (a0 + t0) // 512, sqt, sqb, n))

                def on_end():
                    while pending:
                        emit_acc()

                self.gemm(chunks, tiles, on_tile, kcmax=KC, banks=[0, 1], on_end=on_end)

    def ffn_up(self, es, l, hT, hTb, w_up, FF, bg=None):
        nc = self.nc
        sp = self.sp
        NCH = D_FF // 128
        chunks = []
        for c in range(NCH):
            chunks.append(dict(act=(hT, hTb), kc=DC, M=128, tag=("g", c),
                               pieces=[(self.wview(w_up, DC, c * 128, 128), 128)]))
            chunks.append(dict(act=(hT, hTb), kc=DC, M=128, tag=("v", c),
                               pieces=[(self.wview(w_up, DC, D_FF + c * 128, 128), 128)]))
        stg = self.sbring(es, "fstg", [128, T], F32, 3)
        cvr = self.sbring(es, "fcv", [128, T], F32, 3)
        obr = self.sbring(es, "fob", [128, T], BF16, 2)
        st = {}

        def on_tile(ci, ch, ti, t0, n, ps_ap, psb_):
            if ti == 0:
                st["stg"] = stg.next()
            sg, sgb = st["stg"]
            self.act_fn(sg[:, t0:t0 + n], ps_ap, AF.Copy, [psb_], [sgb])

        def on_chunk(ci, ch):
            kind, c = ch["tag"]
            col = c if kind == "g" else NCH + c
            sg, sgb = st["stg"]
            cvt, cvb = cvr.next()
            self.conv3(cvt[:], cvb, sg[:], sgb, self.vc(l, "ffn_w0", col), self.vc(l, "ffn_w1", col),
                       self.vc(l, "ffn_w2", col), self.vc(l, "ffn_b", col))
            if kind == "g":
                self.act_fn(cvt[:], cvt[:], AF.Silu, [cvb], [cvb])
                st["g"] = (cvt, cvb)
            else:
                gt, gb = st["g"]
                ot, ob = obr.next()
                self.tt(ot[:], gt[:], cvt[:], ALU.mult, [gb, cvb], [ob])
                sp.dma(FF[0][c * 128:(c + 1) * 128, :], ot[:], reads=[ob], writes=[FF[1]])

        self.gemm(chunks, TILES, on_tile, on_chunk, bg=bg)


_CACHE = {}


def _consts():
    if "c" in _CACHE:
        return _CACHE["c"]
    bf = ml_dtypes.bfloat16
    c = {}
    for L in (TS, LP):
        idx = np.arange(L, dtype=np.float64)
        ang = np.pi * np.outer(idx, idx) / L
        Fc = np.cos(ang)
        Fs = -np.sin(ang)
        Fs[:, 0] = (-1.0) ** idx
        Gc = np.cos(ang) / L
        Gc[0, :] = 1.0 / (2 * L)
        Gs = -np.sin(ang) / L
        Gs[0, :] = ((-1.0) ** idx) / (2 * L)
        LCh = L // 128

        def tile_f(M):
            return np.ascontiguousarray(M.astype(np.float32).reshape(LCh, 128, LCh, 128).transpose(2, 1, 0, 3)).astype(bf)

        def tile_g(M):
            w_ = min(512, L)
            return np.ascontiguousarray(M.astype(np.float32).reshape(LCh, 128, L // w_, w_).transpose(2, 1, 0, 3)).astype(bf)

        c[f"Fc{L}"] = tile_f(Fc)
        c[f"Fs{L}"] = tile_f(Fs)
        c[f"Gc{L}"] = tile_g(Gc)
        c[f"Gs{L}"] = tile_g(Gs)
        t_idx = np.arange(L, dtype=np.float32)
        t_norm = t_idx / np.float32(max(L - 1, 1))
        w = (np.float32(2.0 * math.pi) * t_idx / np.float32(L)).astype(np.float32)
        bands = np.linspace(1e-4, 15, 16, dtype=np.float32)
        angz = (w[:, None] * bands[None, :]).astype(np.float32)
        z = np.concatenate([t_norm[:, None], np.cos(angz), -np.sin(angz)], axis=-1).astype(np.float32)
        c[f"zT{L}"] = np.ascontiguousarray(z.T)
        c[f"negtn{L}"] = np.ascontiguousarray((-t_norm).reshape(L // 128, 128).T).astype(np.float32)
    max_decay = math.log(1e-2) / 0.3
    min_decay = math.log(1e-2) / 1.5
    deltas = np.abs(np.linspace(min_decay, max_decay, D_A, dtype=np.float32))
    c["delta_b"] = np.ascontiguousarray(np.broadcast_to(deltas[None, :], (128, D_A))).astype(np.float32)
    c["alt"] = (((-1.0) ** np.arange(128)).reshape(128, 1)).astype(np.float32).astype(bf)
    rows = TS // 64
    row = np.repeat(np.arange(rows, dtype=np.float32), 64)
    col = np.tile(np.arange(64, dtype=np.float32), rows)
    inv = (np.float32(10000.0) ** (-np.arange(16, dtype=np.float32) / np.float32(16))).astype(np.float32)
    ang = np.concatenate([row[:, None] * inv, col[:, None] * inv], axis=-1).astype(np.float32)
    cos, sin = np.cos(ang).T, np.sin(ang).T
    CC = np.ones((ROPE, T), np.float32)
    SS = np.zeros((ROPE, T), np.float32)
    CC[0:32, 0:TS] = cos
    CC[32:64, 0:TS] = cos
    SS[0:32, 0:TS] = -sin
    SS[32:64, 0:TS] = sin
    c["CC"], c["SS"] = CC, SS
    c["ident"] = np.eye(128, dtype=np.float32).astype(bf)
    _CACHE["c"] = c
    return c


def _vec_table(inp):
    tab = np.zeros((128, DEPTH, NVEC), np.float32)
    for l in range(DEPTH):
        src = {"norm_mix_pre": inp["norm_mix_pre"][l], "norm_mix_post": inp["norm_mix_post"][l],
               "norm_ffn_pre": inp["norm_ffn_pre"][l], "norm_ffn_post": inp["norm_ffn_post"][l],
               "ada_b": inp["ada_b"][l], "hy_conv_w0": inp["hy_conv_w"][l, 0], "hy_conv_w1": inp["hy_conv_w"][l, 1],
               "hy_conv_w2": inp["hy_conv_w"][l, 2], "hy_conv_b": inp["hy_conv_b"][l], "q_norm": inp["q_norm"][l],
               "kv_norm": inp["kv_norm"][l], "sc_w0": inp["sc_conv_w"][l, 0], "sc_w1": inp["sc_conv_w"][l, 1],
               "sc_w2": inp["sc_conv_w"][l, 2], "ffn_w0": inp["ffn_conv_w"][l, 0], "ffn_w1": inp["ffn_conv_w"][l, 1],
               "ffn_w2": inp["ffn_conv_w"][l, 2], "ffn_b": inp["ffn_conv_b"][l]}
        for name, k in VEC_SPECS:
            v = np.asarray(src[name], np.float32).reshape(k, 128)
            tab[:, l, VCOL[name]:VCOL[name] + k] = v.T
    return tab


def host_inputs(inp):
    inp = {k: np.asarray(v) for k, v in inp.items()}
    cst = _consts()
    shared = dict(cst)
    for nm in ("ada_w", "w_in", "hy_f_w1", "hy_f_w2", "hy_f_w3", "hy_f_b3", "w_uq", "w_ukv", "w_br_a", "w_br_b",
               "w_br_c", "w_o", "ffn_up", "ffn_down"):
        shared[nm] = np.ascontiguousarray(inp[nm], dtype=np.float32)
    shared["vecT"] = _vec_table(inp)
    hyvec = np.stack([inp["hy_f_b1"], inp["hy_f_b2"], inp["hy_f_freq"][:, 0], inp["hy_f_freq"][:, 1]], axis=-1)
    shared["hyvec"] = np.ascontiguousarray(hyvec, dtype=np.float32)
    shared["hy_bias"] = np.ascontiguousarray(
        np.broadcast_to(inp["hy_bias"][:, None, :], (DEPTH, 128, D_A)), dtype=np.float32)
    maps = []
    for c in range(8):
        m = dict(shared)
        xs = inp["x_sample"][c].T
        xp0 = inp["x_prompt"][2 * c].T
        xp1 = inp["x_prompt"][2 * c + 1].T
        m["xT_in"] = np.ascontiguousarray(np.concatenate([xs, xp0, xp1], axis=1), dtype=np.float32)
        cc = np.stack([inp["c"][c], inp["c_ctx"]], axis=-1)
        m["cT"] = np.ascontiguousarray(cc.reshape(DC, 128, 2).transpose(1, 0, 2), dtype=np.float32)
        ck = inp["cache_ckv"][c]
        m["cckvT"] = np.ascontiguousarray(ck.transpose(0, 2, 1).reshape(DEPTH, 2, 128, PAST).transpose(0, 2, 1, 3),
                                          dtype=np.float32)
        kp = inp["cache_kpe"][c]
        m["ckpeT"] = np.ascontiguousarray(kp.transpose(0, 2, 1), dtype=np.float32)
        maps.append(m)
    return maps


def assemble(results):
    y_prompt = np.zeros((16, LP, D), np.float32)
    y_sample = np.zeros((8, TS, D), np.float32)
    new_ckv = np.zeros((16, DEPTH, LP, KVL), np.float32)
    new_kpe = np.zeros((16, DEPTH, LP, ROPE), np.float32)
    for c in range(8):
        r = results[c]
        yT = np.asarray(r["yT"])
        y_sample[c] = yT[:, 0:TS].T
        y_prompt[2 * c] = yT[:, TS:TS + LP].T
        y_prompt[2 * c + 1] = yT[:, TS + LP:T].T
        ck = np.asarray(r["ckv_out"])
        kp = np.asarray(r["kpe_out"])
        for j in range(2):
            new_ckv[2 * c + j] = ck[:, :, j * LP:(j + 1) * LP].transpose(0, 2, 1)
            new_kpe[2 * c + j] = kp[:, :, j * LP:(j + 1) * LP].transpose(0, 2, 1)
    return (y_prompt, y_sample, new_ckv, new_kpe)


def kernel(**inputs):
    if "nc" not in _CACHE:
        _CACHE["nc"] = MK().build()
    nc = _CACHE["nc"]
    maps = host_inputs(inputs)
    res = run_bass_kernel_spmd(nc, maps, core_ids=list(range(8)))
    return assemble(res.results)
```

```python
import math
from contextlib import ExitStack

import numpy as np
import ml_dtypes

import concourse.bass as bass
import concourse.mybir as mybir
from concourse.bass_utils import run_bass_kernel_spmd

F32 = mybir.dt.float32
BF16 = mybir.dt.bfloat16
AF = mybir.ActivationFunctionType
ALU = mybir.AluOpType

D = 2048
DC = 16
DEPTH = 2
TS = 2048
LP = 256
T = 2560
PAST = 512
NKEY = T + PAST
D_A = 1024
QL = 512
KVL = 256
ROPE = 64
D_FF = 5632
N_IN = 13120
EPS = 1e-6
ATTN_SCALE = 192 ** -0.5
SEGS = [(0, 2048), (2048, 2304), (2304, 2560)]
TILES = [(0, 512), (512, 512), (1024, 512), (1536, 512), (2048, 512)]
TWO_PI = 2.0 * math.pi

VEC_SPECS = [("norm_mix_pre", 16), ("norm_mix_post", 16), ("norm_ffn_pre", 16), ("norm_ffn_post", 16),
             ("ada_b", 96), ("hy_conv_w0", 24), ("hy_conv_w1", 24), ("hy_conv_w2", 24), ("hy_conv_b", 24),
             ("q_norm", 4), ("kv_norm", 2), ("sc_w0", 8), ("sc_w1", 8), ("sc_w2", 8),
             ("ffn_w0", 88), ("ffn_w1", 88), ("ffn_w2", 88), ("ffn_b", 88)]
VCOL = {}
_c = 0
for _n, _k in VEC_SPECS:
    VCOL[_n] = _c
    _c += _k
NVEC = _c


class Buf:
    def __init__(self, name=""):
        self.name = name
        self.w = None
        self.r = {}


class Q:
    def __init__(self, nc, es, eng, name, is_dma=False, ring=8):
        self.nc = nc
        self.eng = eng
        self.name = name
        self.is_dma = is_dma
        self.known = {}
        if is_dma:
            self.ring = [es.enter_context(nc.semaphore(f"{name}_d{i}")) for i in range(ring)]
            self.ring_cnt = [0] * ring
            self.i = 0
        else:
            self.sem = es.enter_context(nc.semaphore(f"{name}_s"))
            self.n = 0

    def wait(self, tok, hazard="raw"):
        if tok is None:
            return
        s, v = tok
        if (not self.is_dma) and s is self.sem and hazard != "raw":
            return
        k = id(s)
        if self.known.get(k, 0) >= v:
            return
        self.eng.wait_ge(s, v)
        self.known[k] = v

    def acquire(self, reads, writes):
        for b in reads:
            self.wait(b.w, "raw")
        for b in writes:
            for t in list(b.r.values()):
                self.wait(t, "war")
            self.wait(b.w, "waw")

    def release(self, tok, reads, writes):
        for b in reads:
            b.r[id(tok[0])] = tok
        for b in writes:
            b.w = tok
            b.r = {}

    def stamp(self, ins):
        self.n += 1
        ins.then_inc(self.sem, 1)
        return (self.sem, self.n)

    def op(self, ins_fn, reads=(), writes=()):
        self.acquire(reads, writes)
        ins = ins_fn()
        tok = self.stamp(ins)
        self.release(tok, reads, writes)
        return tok

    def dma(self, out, in_, reads=(), writes=()):
        slot = self.i % len(self.ring)
        self.i += 1
        s = self.ring[slot]
        if self.ring_cnt[slot] > 0:
            self.wait((s, 16 * self.ring_cnt[slot]))
        self.acquire(reads, writes)
        self.ring_cnt[slot] += 1
        self.eng.dma_start(out=out, in_=in_).then_inc(s, 16)
        tok = (s, 16 * self.ring_cnt[slot])
        self.release(tok, reads, writes)
        return tok

    def all_toks(self):
        if self.is_dma:
            return [(s, 16 * c) for s, c in zip(self.ring, self.ring_cnt) if c > 0]
        return [(self.sem, self.n)] if self.n > 0 else []


class Ring:
    def __init__(self, items):
        self.items = items
        self.i = 0

    def next(self):
        it = self.items[self.i % len(self.items)]
        self.i += 1
        return it


class MK:
    def __init__(self, debug=False, depth=DEPTH):
        self.debug = debug
        self.depth = depth
        self.nc = bass.Bass("TRN2", target_bir_lowering=False)
        self.din = {}
        self.uid = 0

    def inp(self, name, shape, dt=F32):
        t = self.nc.dram_tensor(name, list(shape), dt, kind="ExternalInput").ap()
        self.din[name] = t
        return t

    def outp(self, name, shape, dt=F32):
        return self.nc.dram_tensor(name, list(shape), dt, kind="ExternalOutput").ap()

    def scratch(self, name, shape, dt):
        kind = "ExternalOutput" if self.debug else "Internal"
        return (self.nc.dram_tensor(name, list(shape), dt, kind=kind).ap(), Buf(name))

    def sb(self, es, name, shape, dt):
        self.uid += 1
        t = es.enter_context(self.nc.sbuf_tensor(f"{name}_{self.uid}", list(shape), dt))
        return t, Buf(name)

    def sbring(self, es, name, shape, dt, n):
        return Ring([self.sb(es, f"{name}{i}", shape, dt) for i in range(n)])

    def barrier(self):
        qs = [self.pe, self.act, self.dve, self.sp, self.pool]
        toks = []
        for q in qs:
            toks += q.all_toks()
        for q in qs:
            for t in toks:
                q.wait(t)

    def mm_group(self, ps_ap, ps_buf, pairs, reads):
        pe = self.pe
        pe.acquire(reads, [ps_buf])
        n = len(pairs)
        ins = None
        for i, (l, r) in enumerate(pairs):
            ins = self.nc.tensor.matmul(ps_ap, l, r, start=(i == 0), stop=(i == n - 1))
        tok = pe.stamp(ins)
        pe.release(tok, reads, [ps_buf])
        return tok

    def A(self, ins_fn, reads=(), writes=()):
        return self.act.op(ins_fn, reads, writes)

    def V(self, ins_fn, reads=(), writes=()):
        return self.dve.op(ins_fn, reads, writes)

    def act_fn(self, out, in_, func, reads, writes, bias=None, scale=None):
        kw = {}
        if bias is not None:
            kw["bias"] = bias
        if scale is not None:
            kw["scale"] = scale
        return self.A(lambda: self.nc.scalar.activation(out=out, in_=in_, func=func, **kw), reads, writes)

    def tt(self, out, a, b, op, reads, writes):
        return self.V(lambda: self.nc.vector.tensor_tensor(out=out, in0=a, in1=b, op=op), reads, writes)

    def ts(self, out, a, s1, s2, op0, op1, reads, writes):
        if op1 is None:
            return self.V(lambda: self.nc.vector.tensor_scalar(out=out, in0=a, scalar1=s1, scalar2=None, op0=op0),
                          reads, writes)
        return self.V(lambda: self.nc.vector.tensor_scalar(out=out, in0=a, scalar1=s1, scalar2=s2, op0=op0, op1=op1),
                      reads, writes)

    def stt(self, out, a, s, b, op0, op1, reads, writes):
        return self.V(lambda: self.nc.vector.scalar_tensor_tensor(out=out, in0=a, scalar=s, in1=b, op0=op0, op1=op1),
                      reads, writes)

    def gemm(self, chunks, tiles, on_tile, on_chunk=None, kcmax=16, cast=True, nps=4, bg=None, banks=None,
             on_end=None):
        nc = self.nc
        with ExitStack() as es:
            wring = self.sbring(es, "wr", [128, kcmax, 128], BF16, 3)
            q = self.pool if cast else self.sp
            loaded = {}

            def load(ci):
                if ci >= len(chunks) or ci in loaded:
                    return
                ch = chunks[ci]
                wt, wb = wring.next()
                off = 0
                for view, w in ch["pieces"]:
                    q.dma(wt[:, 0:ch["kc"], off:off + w], view, reads=ch.get("wreads", ()), writes=[wb])
                    off += w
                loaded[ci] = (wt, wb)

            load(0)
            load(1)
            pi = 0
            if bg is not None:
                bgch, bg_tile, bg_per = bg
                bring = self.sbring(es, "bwr", [128, 16, 128], BF16, 2)
                bloaded = {}
                bstate = {"next": 0, "acc": 0.0}

                def bload(bi):
                    if bi >= len(bgch) or bi in bloaded:
                        return
                    bt, bb = bring.next()
                    self.pool.dma(bt[:, 0:bgch[bi]["kc"], :], bgch[bi]["pieces"][0][0], writes=[bb])
                    bloaded[bi] = (bt, bb)

                def bstep():
                    if bstate.get("pending") is not None:
                        bg_tile[1](*bstate.pop("pending"))
                    bi = bstate["next"]
                    if bi >= len(bgch):
                        return
                    bstate["next"] += 1
                    bload(bi)
                    bload(bi + 1)
                    bt, bb = bloaded.pop(bi)
                    bch = bgch[bi]
                    bat, bab = bch["act"]
                    ps_ap = self.ps[6][0:2, 0:128]
                    self.mm_group(ps_ap, self.psb[6], [(bat[:, k, :], bt[:, k, :]) for k in range(bch["kc"])],
                                  [bb, bab])
                    bstate["pending"] = bg_tile[0](bch, ps_ap, self.psb[6])

                bload(0)
                bload(1)
            for ci, ch in enumerate(chunks):
                load(ci + 2)
                if bg is not None:
                    bstate["acc"] += bg_per
                    while bstate["acc"] >= 1.0:
                        bstate["acc"] -= 1.0
                        bstep()
                wt, wb = loaded.pop(ci)
                at, ab = ch["act"]
                M = ch["M"]
                for ti, (t0, n) in enumerate(tiles):
                    p = banks[pi % len(banks)] if banks is not None else pi % nps
                    pi += 1
                    ps_ap = self.ps[p][0:M, 0:n]
                    pairs = [(wt[:, k, 0:M], at[:, k, t0:t0 + n]) for k in range(ch["kc"])]
                    self.mm_group(ps_ap, self.psb[p], pairs, [wb, ab[ti] if isinstance(ab, list) else ab])
                    on_tile(ci, ch, ti, t0, n, ps_ap, self.psb[p])
                if on_chunk is not None:
                    on_chunk(ci, ch)
            if bg is not None:
                while bstate["next"] < len(bgch):
                    bstep()
                bstep()
            if on_end is not None:
                on_end()
            self.barrier()

    def wview(self, w2d, kc, c0, w):
        return w2d.rearrange("(kc p) n -> p kc n", p=128)[:, 0:kc, c0:c0 + w]

    def rstd_from(self, es, get_chunk, nchunks, nfeat, name):
        nc = self.nc
        rstd, rstd_b = self.sb(es, name, [128, T], F32)
        with ExitStack() as es2:
            sq = self.sbring(es2, "sq", [128, T], BF16, 2)
            for k in range(nchunks):
                ap, b = get_chunk(k)
                st, sbf = sq.next()
                self.act_fn(st[:], ap, AF.Square, [b], [sbf])
                for ti, (t0, n) in enumerate(TILES):
                    pe = self.pe
                    pe.acquire([sbf, self.ones_b], [self.psb[ti]])
                    ins = nc.tensor.matmul(self.ps[ti][:, 0:n], self.ones[:], st[:, t0:t0 + n],
                                           start=(k == 0), stop=(k == nchunks - 1))
                    tok = pe.stamp(ins)
                    pe.release(tok, [sbf, self.ones_b], [self.psb[ti]])
            for ti, (t0, n) in enumerate(TILES):
                self.ts(rstd[:, t0:t0 + n], self.ps[ti][:, 0:n], 1.0 / nfeat, EPS, ALU.mult, ALU.add,
                        [self.psb[ti]], [rstd_b])
            self.act_fn(rstd[:], rstd[:], AF.Ln, [rstd_b], [rstd_b])
            self.act_fn(rstd[:], rstd[:], AF.Exp, [rstd_b], [rstd_b], scale=-0.5)
            self.barrier()
        return rstd, rstd_b

    def norm_mod(self, es, x_src, x_srcb, Atab, Btab):
        nc = self.nc
        hT, hTb = self.sb(es, "hT", [128, DC, T], BF16)
        with ExitStack() as es2:
            xr = self.sbring(es2, "xr", [128, T], F32, 3)

            def getx(k):
                xt, xb = xr.next()
                self.sp.dma(xt[:], x_src[k * 128:(k + 1) * 128, :], reads=[x_srcb], writes=[xb])
                return xt[:], xb

            rstd, rstd_b = self.rstd_from(es2, getx, DC, D, "rstd")
            tmpr = self.sbring(es2, "nt", [128, T], F32, 2)
            for k in range(DC):
                xa, xb = getx(k)
                tt_, tb = tmpr.next()
                self.tt(tt_[:], xa, rstd[:], ALU.mult, [xb, rstd_b], [tb])
                for s, (a, b) in ((0, (0, TS)), (1, (TS, T))):
                    self.act_fn(hT[:, k, a:b], tt_[:, a:b], AF.Identity, [tb, self.modb], [hTb],
                                bias=Btab[:, k, s:s + 1], scale=Atab[:, k, s:s + 1])
            self.barrier()
        return hT, hTb

    def post_norm_res(self, OO, x_src, x_srcb, x_dst, x_dstb, Gtab, hbuf, nxt, between=None, acc_banks=None):
        nc = self.nc
        sp = self.sp
        hb = [Buf(f"hb{k}") for k in range(DC)]
        first = OO is None
        with ExitStack() as es:
            if not first:
                for k in range(DC):
                    sp.dma(hbuf[:, k, :], OO[0][k * 128:(k + 1) * 128, :], reads=[OO[1]], writes=[hb[k]])
                if acc_banks is None:
                    rstd, rstd_b = self.rstd_from(es, lambda k: (hbuf[:, k, :], hb[k]), DC, D, "rstd2")
                else:
                    rstd, rstd_b = self.sb(es, "rstd2", [128, T], F32)
                    for ti, (t0, n) in enumerate(TILES):
                        bk = acc_banks[ti]
                        self.ts(rstd[:, t0:t0 + n], self.ps[bk][:, 0:n], 1.0 / D, EPS, ALU.mult, ALU.add,
                                [self.psb[bk]], [rstd_b])
                    self.act_fn(rstd[:], rstd[:], AF.Ln, [rstd_b], [rstd_b])
                    self.act_fn(rstd[:], rstd[:], AF.Exp, [rstd_b], [rstd_b], scale=-0.5)
            else:
                rstd, rstd_b = self.sb(es, "rstd2", [128, T], F32)
            xr = self.sbring(es, "xr2", [128, T], F32, 3)
            dr = xr if first else self.sbring(es, "dr2", [128, T], F32, 3)
            sq = self.sbring(es, "sq2", [128, T], BF16, 2)
            xl = {}

            def load_x(k):
                if k < DC and k not in xl:
                    xt_, xb_ = xr.next()
                    sp.dma(xt_[:], x_src[k * 128:(k + 1) * 128, :], reads=[x_srcb], writes=[xb_])
                    xl[k] = (xt_, xb_)

            load_x(0)
            load_x(1)
            for k in range(DC):
                load_x(k + 2)
                xt, xb = xl.pop(k)
                if first:
                    dt_, db = xt, xb
                else:
                    dt_, db = dr.next()
                    for s, (a, b) in ((0, (0, TS)), (1, (TS, T))):
                        self.stt(dt_[:, a:b], hbuf[:, k, a:b], Gtab[:, k, s:s + 1], rstd[:, a:b], ALU.mult,
                                 ALU.mult, [hb[k], rstd_b, self.modb], [db])
                    self.tt(dt_[:], dt_[:], xt[:], ALU.add, [db, xb], [db])
                    sp.dma(x_dst[k * 128:(k + 1) * 128, :], dt_[:], reads=[db], writes=[x_dstb])
                if nxt is not None:
                    st, sbf = sq.next()
                    self.act_fn(st[:], dt_[:], AF.Square, [db], [sbf])
                    for ti, (t0, n) in enumerate(TILES):
                        pe = self.pe
                        pe.acquire([sbf, self.ones_b], [self.psb[ti]])
                        ins = nc.tensor.matmul(self.ps[ti][:, 0:n], self.ones[:], st[:, t0:t0 + n],
                                               start=(k == 0), stop=(k == DC - 1))
                        tok = pe.stamp(ins)
                        pe.release(tok, [sbf, self.ones_b], [self.psb[ti]])
                    self.act_fn(hbuf[:, k, :], dt_[:], AF.Copy, [db], [hb[k]])
            if between is not None:
                between()
            if nxt is not None:
                Atab, Btab = nxt
                for ti, (t0, n) in enumerate(TILES):
                    self.ts(rstd[:, t0:t0 + n], self.ps[ti][:, 0:n], 1.0 / D, EPS, ALU.mult, ALU.add,
                            [self.psb[ti]], [rstd_b])
                self.act_fn(rstd[:], rstd[:], AF.Ln, [rstd_b], [rstd_b])
                self.act_fn(rstd[:], rstd[:], AF.Exp, [rstd_b], [rstd_b], scale=-0.5)
                for k in range(DC):
                    dt_, db = dr.next()
                    self.tt(dt_[:], hbuf[:, k, :], rstd[:], ALU.mult, [hb[k], rstd_b], [db])
                    for s, (a, b) in ((0, (0, TS)), (1, (TS, T))):
                        self.act_fn(hbuf[:, k, a:b], dt_[:, a:b], AF.Identity, [db, self.modb], [hb[k]],
                                    bias=Btab[:, k, s:s + 1], scale=Atab[:, k, s:s + 1])
            self.barrier()
        return hbuf, Buf("hT")

    def conv3(self, out, outb, src, srcb, w0, w1, w2, bias):
        if bias is not None:
            self.act_fn(out, src, AF.Identity, [srcb, self.vecb], [outb], bias=bias, scale=w1)
        else:
            self.act_fn(out, src, AF.Identity, [srcb, self.vecb], [outb], scale=w1)
        for (a, b) in SEGS:
            self.stt(out[:, a + 1:b], src[:, a:b - 1], w0, out[:, a + 1:b], ALU.mult, ALU.add,
                     [srcb, outb, self.vecb], [outb])
            self.stt(out[:, a:b - 1], src[:, a + 1:b], w2, out[:, a:b - 1], ALU.mult, ALU.add,
                     [srcb, outb, self.vecb], [outb])

    def vc(self, l, name, k):
        c = VCOL[name] + k
        return self.vecT[:, l, c:c + 1]

    def build(self):
        nc = self.nc
        dbg = self.debug
        xT_in = self.inp("xT_in", [D, T])
        cT_in = self.inp("cT", [128, DC, 2])
        cckvT = self.inp("cckvT", [DEPTH, 128, 2, PAST])
        ckpeT = self.inp("ckpeT", [DEPTH, ROPE, PAST])
        vecT_in = self.inp("vecT", [128, DEPTH, NVEC])
        hyvec_in = self.inp("hyvec", [DEPTH, 64, 4])
        hybias_in = self.inp("hy_bias", [DEPTH, 128, D_A])
        W = {}
        for name, shape in [("ada_w", [DEPTH, D, 6 * D]), ("w_in", [DEPTH, D, N_IN]), ("hy_f_w1", [DEPTH, 33, 64]),
                            ("hy_f_w2", [DEPTH, 64, 64]), ("hy_f_w3", [DEPTH, 64, 2 * D_A]),
                            ("hy_f_b3", [DEPTH, 2 * D_A]), ("w_uq", [DEPTH, QL, 1536]),
                            ("w_ukv", [DEPTH, KVL, 2048]), ("w_br_a", [DEPTH, 1024, D]),
                            ("w_br_b", [DEPTH, 1024, D]), ("w_br_c", [DEPTH, 1024, D]), ("w_o", [DEPTH, D, D]),
                            ("ffn_up", [DEPTH, D, 2 * D_FF]), ("ffn_down", [DEPTH, D_FF, D])]:
            W[name] = self.inp(name, shape)
        C = {}
        for L in (TS, LP):
            for nm in ("Fc", "Fs"):
                C[f"{nm}{L}"] = self.inp(f"{nm}{L}", [L // 128, 128, L // 128, 128], BF16)
            for nm in ("Gc", "Gs"):
                C[f"{nm}{L}"] = self.inp(f"{nm}{L}", [max(1, L // 512), 128, L // 128, min(512, L)], BF16)
            C[f"zT{L}"] = self.inp(f"zT{L}", [33, L])
            C[f"negtn{L}"] = self.inp(f"negtn{L}", [128, L // 128])
        C["delta_b"] = self.inp("delta_b", [128, D_A])
        C["alt"] = self.inp("alt", [128, 1], BF16)
        C["CC"] = self.inp("CC", [ROPE, T])
        C["SS"] = self.inp("SS", [ROPE, T])
        C["ident"] = self.inp("ident", [128, 128], BF16)

        yT = self.outp("yT", [D, T])
        ckv_out = self.outp("ckv_out", [DEPTH, KVL, 2 * LP])
        kpe_out = self.outp("kpe_out", [DEPTH, ROPE, 2 * LP])
        yT_b, ckv_out_b, kpe_out_b = Buf(), Buf(), Buf()

        xa_s = self.scratch("x_a", [D, T], F32)
        xb_s = self.scratch("x_b", [D, T], F32)
        X0 = self.scratch("X0", [D_A, T], BF16)
        ZZ = self.scratch("ZZ", [D_A, T], BF16)
        SIG = self.scratch("SIG", [3 * D, T], BF16)
        YA = self.scratch("YA", [D_A, T], BF16)
        YB = self.scratch("YB", [D_A, T], BF16)
        YC = self.scratch("YC", [D_A, T], BF16)
        MM = self.scratch("MM", [D, T], BF16)
        OO = self.scratch("OO", [D, T], BF16)
        FF = self.scratch("FF", [D_FF, T], BF16)
        RAW = self.scratch("RAW", [1024, T], F32)
        xin_b = Buf("xin")
        cin_b = Buf("cin")

        with ExitStack() as es:
            self.pe = Q(nc, es, nc.tensor, "pe")
            self.act = Q(nc, es, nc.scalar, "act")
            self.dve = Q(nc, es, nc.vector, "dve")
            self.sp = Q(nc, es, nc.sync, "sp", is_dma=True)
            self.pool = Q(nc, es, nc.gpsimd, "pool", is_dma=True)
            sp, pool = self.sp, self.pool
            self.ps = [es.enter_context(nc.psum_tensor(f"ps{i}", [128, 512], F32)) for i in range(7)]
            self.psb = [Buf(f"ps{i}") for i in range(7)]
            self.pst = es.enter_context(nc.psum_tensor("pst", [128, 1024], BF16))
            self.pstb = Buf("pst")
            self.ones, self.ones_b = self.sb(es, "ones", [128, 128], BF16)
            self.V(lambda: nc.vector.memset(self.ones[:], 1.0), [], [self.ones_b])
            self.ident, self.ident_b = self.sb(es, "ident", [128, 128], BF16)
            sp.dma(self.ident[:], C["ident"][:], writes=[self.ident_b])
            self.vecT, self.vecb = self.sb(es, "vecT", [128, DEPTH, NVEC], F32)
            sp.dma(self.vecT[:], vecT_in[:], writes=[self.vecb])
            self.modtab, self.modb = self.sb(es, "modtab", [128, DEPTH, 6, DC, 2], F32)

            cT, cTb = self.sb(es, "cT", [128, DC, 2], F32)
            sp.dma(cT[:], cT_in[:], writes=[cTb])
            scT, scTb = self.sb(es, "scT", [128, DC, 2], BF16)
            self.act_fn(scT[:], cT[:], AF.Silu, [cTb], [scTb])
            modraw, modrawb = self.sb(es, "modraw", [128, DEPTH, 96, 2], F32)

            def ada_chunk(l, m):
                return dict(act=(scT, scTb), kc=DC, M=128, tag=(l, m),
                            pieces=[(self.wview(W["ada_w"][l], DC, m * 128, 128), 128)])

            def ada_tile(ch, ps_ap, psb):
                l_, m = ch["tag"]
                self.ts(modraw[:, l_, m, :], ps_ap, self.vc(l_, "ada_b", m), None, ALU.add, None,
                        [psb, self.vecb], [modrawb])

            ident2, ident2b = self.sb(es, "ident2", [2, 2], F32)
            self.V(lambda: nc.vector.tensor_copy(out=ident2[:], in_=self.ident[0:2, 0:2]), [self.ident_b], [ident2b])
            mrow = self.sbring(es, "mrow", [2, 128], F32, 2)

            def ada_tile_bg_a(ch, ps_row, psb):
                rt, rb = mrow.next()
                self.act_fn(rt[:], ps_row, AF.Copy, [psb], [rb])
                return (ch, rt, rb)

            def ada_tile_bg_b(ch, rt, rb):
                l_, m = ch["tag"]
                self.mm_group(self.ps[5][:, 0:2], self.psb[5], [(rt[:], ident2[:])], [rb, ident2b])
                self.ts(modraw[:, l_, m, :], self.ps[5][:, 0:2], self.vc(l_, "ada_b", m), None, ALU.add, None,
                        [self.psb[5], self.vecb], [modrawb])

            def mod_tables(l_, which):
                mt = self.modtab
                for s_ in range(2):
                    for k in range(DC):
                        for (ai, bi, gi, sh_c, sc_c, g_c, npre, npost) in (
                                (0, 1, 2, 0, 16, 32, "norm_mix_pre", "norm_mix_post"),
                                (3, 4, 5, 48, 64, 80, "norm_ffn_pre", "norm_ffn_post")):
                            if "ab" in which and ai == 0 or "rest" in which and ai == 3:
                                self.ts(mt[:, l_, ai, k, s_:s_ + 1], modraw[:, l_, sc_c + k, s_:s_ + 1], 1.0,
                                        self.vc(l_, npre, k), ALU.add, ALU.mult, [modrawb, self.vecb], [self.modb])
                                self.V(lambda: nc.vector.tensor_copy(out=mt[:, l_, bi, k, s_:s_ + 1],
                                                                     in_=modraw[:, l_, sh_c + k, s_:s_ + 1]),
                                       [modrawb], [self.modb])
                            if "rest" in which:
                                self.ts(mt[:, l_, gi, k, s_:s_ + 1], modraw[:, l_, g_c + k, s_:s_ + 1],
                                        self.vc(l_, npost, k), None, ALU.mult, None, [modrawb, self.vecb],
                                        [self.modb])

            def ada_fg():
                self.gemm([ada_chunk(0, m) for m in range(32)], [(0, 2)],
                          lambda ci, ch, ti, t0, n, ps_ap, psb: ada_tile(ch, ps_ap, psb), banks=[5, 6])
                mod_tables(0, ("ab",))

            bg_chunks = [ada_chunk(0, m) for m in range(32, 96)]
            for l_ in range(1, self.depth):
                bg_chunks += [ada_chunk(l_, m) for m in range(96)]
            self.ada_bg = (bg_chunks[:104], (ada_tile_bg_a, ada_tile_bg_b), 1.0)
            self.ada_bg2 = (bg_chunks[104:], (ada_tile_bg_a, ada_tile_bg_b), 1.0) if len(bg_chunks) > 104 else None
            self.mod_tables = mod_tables

            x_cur, x_curb = xT_in, xin_b
            mt = self.modtab

            def open_layer(l):
                esl = ExitStack()
                cqn, cqnb = self.sb(esl, "cqn", [128, 4, T], BF16)
                keysT, keysTb = self.sb(esl, "keysT", [128, 2, NKEY], BF16)
                kperT, kperTb = self.sb(esl, "kperT", [ROPE, NKEY], BF16)
                pool.dma(keysT[:, :, T:NKEY], cckvT[l], writes=[keysTb])
                pool.dma(kperT[:, T:NKEY], ckpeT[l], writes=[kperTb])
                return esl, (cqn, cqnb, keysT, keysTb, kperT, kperTb)

            esl, att = open_layer(0)
            esh = ExitStack()
            hb0 = self.sb(esh, "hbuf", [128, DC, T], BF16)
            hT, hTb = self.post_norm_res(None, x_cur, x_curb, None, None, None, hb0[0],
                                         (mt[:, 0, 0], mt[:, 0, 1]), between=ada_fg)
            for l in range(self.depth):
                last = (l == self.depth - 1)
                x_mid, x_midb = xa_s
                x_nxt, x_nxtb = (yT, yT_b) if last else xb_s
                G1 = mt[:, l, 2]
                A2, B2, G2 = mt[:, l, 3], mt[:, l, 4], mt[:, l, 5]
                cqn, cqnb, keysT, keysTb, kperT, kperTb = att
                self.inproj(l, esh, hT, hTb, W["w_in"][l], X0, ZZ, SIG, YC, RAW, cqn, cqnb, keysT,
                            keysTb, kperT, kperTb, ckv_out, ckv_out_b, kpe_out, kpe_out_b, C,
                            bg=self.ada_bg if l == 0 else None)
                if l == 0:
                    self.mod_tables(0, ("rest",))
                    self.barrier()
                self.attention(l, W["w_uq"][l], W["w_ukv"][l], cqn, cqnb, keysT, keysTb, kperT, kperTb, YB, C)
                esl.close()
                self.hyena(l, W, C, hyvec_in, hybias_in, X0, ZZ, YA)
                self.merge(l, W, SIG, YA, YB, YC, MM)
                self.outproj(W["w_o"][l], DC, MM, OO, [TILES])
                with ExitStack() as es2:
                    hb2 = self.sb(es2, "hbuf", [128, DC, T], BF16)
                    hT2, hT2b = self.post_norm_res(OO, x_cur, x_curb, x_mid, x_midb, G1, hb2[0], (A2, B2),
                                                   acc_banks=[2, 3, 4, 5, 6])
                    self.ffn_up(es2, l, hT2, hT2b, W["ffn_up"][l], FF, bg=self.ada_bg2 if l == 0 else None)
                    if l == 0:
                        for l_ in range(1, self.depth):
                            self.mod_tables(l_, ("ab", "rest"))
                        self.barrier()
                self.outproj(W["ffn_down"][l], D_FF // 128, FF, OO, [TILES[0:3], TILES[3:5]])
                if not last:
                    esl, att = open_layer(l + 1)
                    esh = ExitStack()
                    hb1 = self.sb(esh, "hbuf", [128, DC, T], BF16)
                    hT, hTb = self.post_norm_res(OO, x_mid, x_midb, x_nxt, x_nxtb, G2, hb1[0],
                                                 (mt[:, l + 1, 0], mt[:, l + 1, 1]), acc_banks=[2, 3, 4, 5, 6])
                else:
                    with ExitStack() as es3:
                        hb3 = self.sb(es3, "hbuf", [128, DC, T], BF16)
                        self.post_norm_res(OO, x_mid, x_midb, x_nxt, x_nxtb, G2, hb3[0], None,
                                           acc_banks=[2, 3, 4, 5, 6])
                x_cur, x_curb = x_nxt, x_nxtb
            self.barrier()
        return nc

    def inproj(self, l, esh, hT, hTb, w_in, X0, ZZ, SIG, YC, RAW, cqn, cqnb, keysT, keysTb, kperT, kperTb,
               ckv_out, ckv_out_b, kpe_out, kpe_out_b, C, bg=None):
        nc = self.nc
        sp = self.sp
        with esh as es:
            act = (hT, hTb)

            def mk(kind, idx, c0, M=128, pieces=None):
                if pieces is None:
                    pieces = [(self.wview(w_in, DC, c0, M), M)]
                return dict(act=act, kc=DC, M=M, tag=(kind, idx), pieces=pieces)

            chunks = []
            for c in range(4):
                chunks.append(mk("raw", c, 3072 + c * 128))
            for c in range(2):
                chunks.append(mk("raw", 4 + c, 3584 + c * 128))
            chunks.append(mk("raw", 6, 3840, M=64))
            chunks.append(mk("raw", 7, 0, M=64, pieces=[(self.wview(w_in, DC, 3872, 32), 32),
                                                         (self.wview(w_in, DC, 3840, 32), 32)]))
            for c in range(8):
                chunks += [mk("x0", c, c * 128), mk("x1", c, 1024 + c * 128), mk("v", c, 2048 + c * 128)]
            for c in range(8):
                chunks += [mk("scb", c, 3904 + c * 128), mk("scc", c, 4928 + c * 128), mk("scu", c, 5952 + c * 128)]
            for c in range(48):
                chunks.append(mk("gate", c, 6976 + c * 128))

            stg = self.sbring(es, "stg", [128, T], F32, 3)
            cvr = self.sbring(es, "cv", [128, T], F32, 2)
            obr = self.sbring(es, "ob", [128, T], BF16, 2)
            st = {}

            def on_tile(ci, ch, ti, t0, n, ps_ap, psb):
                kind, idx = ch["tag"]
                M = ch["M"]
                if kind == "gate":
                    if ti == 0:
                        st["ob"] = obr.next()
                    ot, ob_ = st["ob"]
                    self.act_fn(ot[:, t0:t0 + n], ps_ap, AF.Sigmoid, [psb], [ob_])
                else:
                    if ti == 0:
                        st["stg"] = stg.next()
                    sg, sgb = st["stg"]
                    self.act_fn(sg[0:M, t0:t0 + n], ps_ap, AF.Copy, [psb], [sgb])

            def on_chunk(ci, ch):
                kind, idx = ch["tag"]
                M = ch["M"]
                if kind == "gate":
                    ot, ob_ = st["ob"]
                    sp.dma(SIG[0][idx * 128:(idx + 1) * 128, :], ot[:], reads=[ob_], writes=[SIG[1]])
                elif kind == "raw":
                    sg, sgb = st["stg"]
                    sp.dma(RAW[0][idx * 128:idx * 128 + M, :], sg[0:M, :], reads=[sgb], writes=[RAW[1]])
                elif kind in ("x0", "x1", "v"):
                    sg, sgb = st["stg"]
                    col = {"x0": 0, "x1": 8, "v": 16}[kind] + idx
                    cvt, cvb = cvr.next()
                    self.conv3(cvt[:], cvb, sg[:], sgb, self.vc(l, "hy_conv_w0", col),
                               self.vc(l, "hy_conv_w1", col), self.vc(l, "hy_conv_w2", col),
                               self.vc(l, "hy_conv_b", col))
                    if kind == "x0":
                        ot, ob_ = obr.next()
                        self.act_fn(ot[:], cvt[:], AF.Copy, [cvb], [ob_])
                        sp.dma(X0[0][idx * 128:(idx + 1) * 128, :], ot[:], reads=[ob_], writes=[X0[1]])
                    elif kind == "x1":
                        st["x1"] = (cvt, cvb)
                    else:
                        x1t, x1b = st["x1"]
                        ot, ob_ = obr.next()
                        self.tt(ot[:], x1t[:], cvt[:], ALU.mult, [x1b, cvb], [ob_])
                        sp.dma(ZZ[0][idx * 128:(idx + 1) * 128, :], ot[:], reads=[ob_], writes=[ZZ[1]])
                elif kind == "scb":
                    st["scb"] = st["stg"]
                elif kind == "scc":
                    st["scc"] = st["stg"]
                elif kind == "scu":
                    sg, sgb = st["stg"]
                    cgt, cgb = st["scc"]
                    bt, bb = st["scb"]
                    self.tt(sg[:], sg[:], cgt[:], ALU.mult, [sgb, cgb], [sgb])
                    cvt, cvb = cvr.next()
                    self.conv3(cvt[:], cvb, sg[:], sgb, self.vc(l, "sc_w0", idx), self.vc(l, "sc_w1", idx),
                               self.vc(l, "sc_w2", idx), None)
                    ot, ob_ = obr.next()
                    self.tt(ot[:], cvt[:], bt[:], ALU.mult, [cvb, bb], [ob_])
                    sp.dma(YC[0][idx * 128:(idx + 1) * 128, :], ot[:], reads=[ob_], writes=[YC[1]])

            self.gemm(chunks, TILES, on_tile, on_chunk, bg=bg)
        with ExitStack() as es2:
            cq_raw, cq_rawb = self.sb(es2, "cq_raw", [128, 4, T], F32)
            ckv_raw, ckv_rawb = self.sb(es2, "ckv_raw", [128, 2, T], F32)
            kpe_raw, kpe_rawb = self.sb(es2, "kpe_raw", [ROPE, T], F32)
            ksw_raw, ksw_rawb = self.sb(es2, "ksw_raw", [ROPE, T], F32)
            cc, ccb = self.sb(es2, "cc", [ROPE, T], F32)
            ss, ssb = self.sb(es2, "ss", [ROPE, T], F32)
            sp.dma(cq_raw[:], RAW[0][0:512, :].rearrange("(c p) t -> p c t", p=128), reads=[RAW[1]], writes=[cq_rawb])
            sp.dma(ckv_raw[:], RAW[0][512:768, :].rearrange("(c p) t -> p c t", p=128), reads=[RAW[1]],
                   writes=[ckv_rawb])
            sp.dma(kpe_raw[:], RAW[0][768:832, :], reads=[RAW[1]], writes=[kpe_rawb])
            sp.dma(ksw_raw[:], RAW[0][896:960, :], reads=[RAW[1]], writes=[ksw_rawb])
            sp.dma(cc[:], C["CC"][:], writes=[ccb])
            sp.dma(ss[:], C["SS"][:], writes=[ssb])
            sp.dma(kpe_out[l], kpe_raw[:, TS:T], reads=[kpe_rawb], writes=[kpe_out_b])
            self.tt(ksw_raw[:], ksw_raw[:], ss[:], ALU.mult, [ksw_rawb, ssb], [ksw_rawb])
            self.tt(cc[:], kpe_raw[:], cc[:], ALU.mult, [kpe_rawb, ccb], [ccb])
            self.tt(kperT[:, 0:T], cc[:], ksw_raw[:], ALU.add, [ccb, ksw_rawb], [kperTb])
            rq, rqb = self.rstd_from(es2, lambda k: (cq_raw[:, k, :], cq_rawb), 4, QL, "rq")
            for k in range(4):
                self.stt(cqn[:, k, :], cq_raw[:, k, :], self.vc(l, "q_norm", k), rq[:], ALU.mult, ALU.mult,
                         [cq_rawb, rqb, self.vecb], [cqnb])
            rk, rkb = self.rstd_from(es2, lambda k: (ckv_raw[:, k, :], ckv_rawb), 2, KVL, "rk")
            for k in range(2):
                self.stt(ckv_raw[:, k, :], ckv_raw[:, k, :], self.vc(l, "kv_norm", k), rk[:], ALU.mult, ALU.mult,
                         [ckv_rawb, rkb, self.vecb], [ckv_rawb])
                self.act_fn(keysT[:, k, 0:T], ckv_raw[:, k, :], AF.Copy, [ckv_rawb], [keysTb])
                sp.dma(ckv_out[l, k * 128:(k + 1) * 128, :], ckv_raw[:, k, TS:T], reads=[ckv_rawb],
                       writes=[ckv_out_b])
            self.barrier()

    def attention(self, l, w_uq, w_ukv, cqn, cqnb, keysT, keysTb, kperT, kperTb, YB, C):
        nc = self.nc
        sp, pool = self.sp, self.pool
        with ExitStack() as es:
            wuq, wuqb = self.sb(es, "wuq", [128, 4, 1536], BF16)
            pool.dma(wuq[:], w_uq.rearrange("(kc p) n -> p kc n", p=128), writes=[wuqb])
            wsw, wswb = self.sb(es, "wsw", [128, 4, 8, 64], BF16)
            v4 = w_uq.rearrange("(kc p) (h d) -> p kc h d", p=128, d=192)
            for kc in range(4):
                pool.dma(wsw[:, kc, :, 0:32], v4[:, kc, :, 160:192], writes=[wswb])
                pool.dma(wsw[:, kc, :, 32:64], v4[:, kc, :, 128:160], writes=[wswb])
            wkv, wkvb = self.sb(es, "wkv", [128, 2, 2048], BF16)
            pool.dma(wkv[:], w_ukv.rearrange("(kc p) n -> p kc n", p=128), writes=[wkvb])
            cc, ccb = self.sb(es, "cc2", [ROPE, T], F32)
            ss, ssb = self.sb(es, "ss2", [ROPE, T], F32)
            sp.dma(cc[:], C["CC"][:], writes=[ccb])
            sp.dma(ss[:], C["SS"][:], writes=[ssb])
            qn, qnb = self.sb(es, "qn", [128, T], BF16)
            qp, qpb = self.sb(es, "qp", [ROPE, T], BF16)
            kh, khb = self.sb(es, "kh", [128, NKEY], BF16)
            vh, vhb = self.sb(es, "vh", [128, NKEY // 128, 128], BF16)
            t1, t1b = self.sb(es, "t1", [ROPE, 512], F32)
            t2, t2b = self.sb(es, "t2", [ROPE, 512], F32)
            ptr = self.sbring(es, "pt", [128, 512], BF16, 4)
            rden, rdenb = self.sb(es, "rden", [128, 512], F32)
            ybr = self.sbring(es, "yb", [128, 512], BF16, 3)

            ps, psb = self.ps, self.psb
            sring = Ring([0, 1, 2])
            oi = 0
            for h in range(8):
                for (t0, n) in TILES:
                    p = sring.next()
                    self.mm_group(ps[p][:, 0:n], psb[p],
                                  [(wuq[:, k, h * 192:h * 192 + 128], cqn[:, k, t0:t0 + n]) for k in range(4)],
                                  [wuqb, cqnb])
                    self.act_fn(qn[:, t0:t0 + n], ps[p][:, 0:n], AF.Copy, [psb[p]], [qnb])
                    p = sring.next()
                    self.mm_group(ps[p][0:ROPE, 0:n], psb[p],
                                  [(wuq[:, k, h * 192 + 128:h * 192 + 192], cqn[:, k, t0:t0 + n]) for k in range(4)],
                                  [wuqb, cqnb])
                    self.tt(t1[:, 0:n], ps[p][0:ROPE, 0:n], cc[:, t0:t0 + n], ALU.mult, [psb[p], ccb], [t1b])
                    p = sring.next()
                    self.mm_group(ps[p][0:ROPE, 0:n], psb[p],
                                  [(wsw[:, k, h, :], cqn[:, k, t0:t0 + n]) for k in range(4)], [wswb, cqnb])
                    self.tt(t2[:, 0:n], ps[p][0:ROPE, 0:n], ss[:, t0:t0 + n], ALU.mult, [psb[p], ssb], [t2b])
                    self.tt(qp[:, t0:t0 + n], t1[:, 0:n], t2[:, 0:n], ALU.add, [t1b, t2b], [qpb])
                for kt in range(NKEY // 512):
                    p = sring.next()
                    self.mm_group(ps[p][:, :], psb[p],
                                  [(wkv[:, k, h * 256:h * 256 + 128], keysT[:, k, kt * 512:(kt + 1) * 512])
                                   for k in range(2)], [wkvb, keysTb])
                    self.act_fn(kh[:, kt * 512:(kt + 1) * 512], ps[p][:, :], AF.Copy, [psb[p]], [khb])
                for g in range(NKEY // 512):
                    p = sring.next()
                    for j in range(4):
                        kt = g * 4 + j
                        self.mm_group(ps[p][:, j * 128:(j + 1) * 128], psb[p],
                                      [(keysT[:, k, kt * 128:(kt + 1) * 128], wkv[:, k, h * 256 + 128:h * 256 + 256])
                                       for k in range(2)], [wkvb, keysTb])
                    self.V(lambda: nc.vector.tensor_copy(
                        out=vh[:, g * 4:(g + 1) * 4, :], in_=ps[p][:, :].rearrange("p (j v) -> p j v", v=128)),
                        [psb[p]], [vhb])
                qtiles = [(0, 512, list(range(0, 16)) + list(range(20, 24))),
                          (512, 512, list(range(0, 16)) + list(range(20, 24))),
                          (1024, 512, list(range(0, 16)) + list(range(20, 24))),
                          (1536, 512, list(range(0, 16)) + list(range(20, 24))),
                          (2048, 256, [16, 17]), (2304, 256, [18, 19])]
                for (q0, nq, kts) in qtiles:
                    po = 3 + (oi % 2)
                    pd = 5 + (oi % 2)
                    oi += 1

                    def qk(kt):
                        p = sring.next()
                        self.mm_group(ps[p][:, 0:nq], psb[p],
                                      [(kh[:, kt * 128:(kt + 1) * 128], qn[:, q0:q0 + nq]),
                                       (kperT[:, kt * 128:(kt + 1) * 128], qp[:, q0:q0 + nq])],
                                      [khb, qnb, kperTb, qpb])
                        return p

                    pcur = qk(kts[0])
                    nk = len(kts)
                    self.pe.acquire([], [psb[po], psb[pd]])
                    for i, kt in enumerate(kts):
                        pnext = qk(kts[i + 1]) if i + 1 < nk else None
                        pt, ptb = ptr.next()
                        self.act_fn(pt[:, 0:nq], ps[pcur][:, 0:nq], AF.Exp, [psb[pcur]], [ptb], scale=ATTN_SCALE)
                        self.pe.acquire([ptb, vhb, self.ones_b], [])
                        nc.tensor.matmul(ps[po][:, 0:nq], vh[:, kt, :], pt[:, 0:nq], start=(i == 0),
                                         stop=(i == nk - 1))
                        ins = nc.tensor.matmul(ps[pd][:, 0:nq], self.ones[:], pt[:, 0:nq], start=(i == 0),
                                               stop=(i == nk - 1))
                        tok = self.pe.stamp(ins)
                        self.pe.release(tok, [ptb, vhb, self.ones_b], [psb[po], psb[pd]] if i == nk - 1 else [])
                        pcur = pnext
                    self.V(lambda: nc.vector.reciprocal(out=rden[:, 0:nq], in_=ps[pd][:, 0:nq]), [psb[pd]], [rdenb])
                    ybt, ybb = ybr.next()
                    self.tt(ybt[:, 0:nq], ps[po][:, 0:nq], rden[:, 0:nq], ALU.mult, [psb[po], rdenb], [ybb])
                    sp.dma(YB[0][h * 128:(h + 1) * 128, q0:q0 + nq], ybt[:, 0:nq], reads=[ybb], writes=[YB[1]])
            self.barrier()

    def hyena(self, l, W, C, hyvec_in, hybias_in, X0, ZZ, YA):
        nc = self.nc
        sp, pool = self.sp, self.pool
        ps, psb = self.ps, self.psb
        with ExitStack() as es:
            w1, w1b = self.sb(es, "hw1", [33, 64], F32)
            w2, w2b = self.sb(es, "hw2", [64, 64], F32)
            w3, w3b = self.sb(es, "hw3", [65, 2 * D_A], BF16)
            hv, hvb = self.sb(es, "hv", [64, 4], F32)
            hb, hbb = self.sb(es, "hb", [128, D_A], F32)
            delta, deltab = self.sb(es, "delta", [128, D_A], F32)
            alt, altb = self.sb(es, "alt", [128, 1], BF16)
            sp.dma(w1[:], W["hy_f_w1"][l], writes=[w1b])
            sp.dma(w2[:], W["hy_f_w2"][l], writes=[w2b])
            pool.dma(w3[0:64, :], W["hy_f_w3"][l], writes=[w3b])
            pool.dma(w3[64:65, :], W["hy_f_b3"][l:l + 1, :], writes=[w3b])
            sp.dma(hv[:], hyvec_in[l], writes=[hvb])
            sp.dma(hb[:], hybias_in[l], writes=[hbb])
            sp.dma(delta[:], C["delta_b"][:], writes=[deltab])
            sp.dma(alt[:], C["alt"][:], writes=[altb])
            fb, fbb = self.sb(es, "fb", [64, 2], F32)
            for i in range(2):
                self.ts(fb[:, i:i + 1], hv[:, i:i + 1], hv[:, 2 + i:3 + i], None, ALU.mult, None, [hvb], [fbb])
            for (L, seqs) in ((TS, [0]), (LP, [TS, TS + LP])):
                LC = L // 128
                with ExitStack() as esL:
                    Kre, Kreb = self.sb(esL, "Kre", [128, LC, D_A], BF16)
                    Kim, Kimb = self.sb(esL, "Kim", [128, LC, D_A], BF16)
                    self.hy_filter(l, L, LC, C, w1, w1b, w2, w2b, w3, w3b, hv, hvb, fb, fbb, hb, hbb, delta, deltab,
                                   alt, altb, Kre, Kreb, Kim, Kimb)
                    if L == TS:
                        for half in range(2):
                            self.hy_conv(L, LC, C, [(seqs[0], half * 512)], Kre, Kreb, Kim, Kimb, X0, ZZ, YA)
                    else:
                        self.hy_conv(L, LC, C, [(s0, half * 512) for s0 in seqs for half in range(2)], Kre, Kreb,
                                     Kim, Kimb, X0, ZZ, YA)
            self.barrier()

    def hy_filter(self, l, L, LC, C, w1, w1b, w2, w2b, w3, w3b, hv, hvb, fb, fbb, hb, hbb, delta, deltab, alt, altb,
                  Kre, Kreb, Kim, Kimb):
        nc = self.nc
        sp = self.sp
        ps, psb = self.ps, self.psb
        with ExitStack() as es:
            ntn, ntnb = self.sb(es, "ntn", [128, LC], F32)
            sp.dma(ntn[:], C[f"negtn{L}"][:], writes=[ntnb])
            h2, h2b = self.sb(es, "h2", [65, L], BF16)
            self.V(lambda: nc.vector.memset(h2[64:65, :], 1.0), [], [h2b])
            with ExitStack() as esz:
                zT, zTb = self.sb(esz, "zT", [33, L], F32)
                sp.dma(zT[:], C[f"zT{L}"][:], writes=[zTb])
                h1, h1b = self.sb(esz, "h1", [64, L], F32)
                ya, yab = self.sb(esz, "ysin", [64, 512], F32)
                yk, ykb = self.sb(esz, "ysink", [64, 512], mybir.dt.int32)
                yf, yfb = self.sb(esz, "ysinf", [64, 512], F32)
                tl = [(t0, min(512, L - t0)) for t0 in range(0, L, 512)]
                for (wt, wb, src, srcb, dst, dstb, i) in ((w1, w1b, zT, zTb, h1, h1b, 0),
                                                          (w2, w2b, h1, h1b, h2, h2b, 1)):
                    for (t0, n) in tl:
                        self.mm_group(ps[0][0:64, 0:n], psb[0], [(wt[:], src[:, t0:t0 + n])], [wb, srcb])
                        self.ts(ya[:, 0:n], ps[0][0:64, 0:n], hv[:, 2 + i:3 + i], fb[:, i:i + 1], ALU.mult, ALU.add,
                                [psb[0], hvb, fbb], [yab])
                        self.ts(yk[:, 0:n], ya[:, 0:n], 1.0 / TWO_PI, None, ALU.mult, None, [yab], [ykb])
                        self.V(lambda: nc.vector.tensor_copy(out=yf[:, 0:n], in_=yk[:, 0:n]), [ykb], [yfb])
                        self.stt(ya[:, 0:n], yf[:, 0:n], -TWO_PI, ya[:, 0:n], ALU.mult, ALU.add, [yfb, yab], [yab])
                        self.ts(ya[:, 0:n], ya[:, 0:n], math.pi - 1e-5, -math.pi + 1e-5, ALU.min, ALU.max,
                                [yab], [yab])
                        self.act_fn(dst[0:64, t0:t0 + n], ya[:, 0:n], AF.Sin, [yab], [dstb])
                self.barrier()
            hsT, hsTb = self.sb(es, "hsT", [128, LC, D_A], BF16)
            hdT, hdTb = self.sb(es, "hdT", [128, LC, D_A], BF16)
            winr = self.sbring(es, "win", [128, D_A], F32, 2)
            ab2, ab2b = self.sb(es, "ab2", [128, D_A], F32)
            hf, hfb = self.sb(es, "hf", [128, D_A], F32)
            hbw, hbwb = self.sb(es, "hbw", [128, D_A], F32)
            ab1, ab1b = self.sb(es, "ab1", [128, D_A], F32)
            ab, abb = self.sb(es, "ab", [128, D_A], BF16)
            for i in range(LC):
                for j in range(4):
                    self.mm_group(ps[j][:, :], psb[j], [(h2[:, i * 128:(i + 1) * 128], w3[:, j * 512:(j + 1) * 512])],
                                  [h2b, w3b])
                win, winb = winr.next()
                self.act_fn(win[:], delta[:], AF.Exp, [deltab, ntnb], [winb], scale=ntn[:, i:i + 1])
                for j in range(2):
                    self.tt(hf[:, j * 512:(j + 1) * 512], ps[j][:, :], win[:, j * 512:(j + 1) * 512], ALU.mult,
                            [psb[j], winb], [hfb])
                    self.tt(hbw[:, j * 512:(j + 1) * 512], ps[2 + j][:, :], win[:, j * 512:(j + 1) * 512], ALU.mult,
                            [psb[2 + j], winb], [hbwb])
                if i == 0:
                    self.V(lambda: nc.vector.memset(hbw[0:1, :], 0.0), [], [hbwb])
                self.tt(hsT[:, i, :], hf[:], hbw[:], ALU.add, [hfb, hbwb], [hsTb])
                self.tt(hdT[:, i, :], hf[:], hbw[:], ALU.subtract, [hfb, hbwb], [hdTb])
                self.act_fn(ab1[:], hbw[:], AF.Abs, [hbwb], [ab1b])
                self.act_fn(ab2[:], hf[:], AF.Abs, [hfb], [ab2b])
                self.tt(ab[:], ab1[:], ab2[:], ALU.add, [ab1b, ab2b], [abb])
                for j in range(2):
                    pe = self.pe
                    pe.acquire([abb, self.ones_b], [psb[4 + j]])
                    ins = nc.tensor.matmul(ps[4 + j][:, :], self.ones[:], ab[:, j * 512:(j + 1) * 512],
                                           start=(i == 0), stop=(i == LC - 1))
                    tok = pe.stamp(ins)
                    pe.release(tok, [abb, self.ones_b], [psb[4 + j]])
            rn, rnb = self.sb(es, "rn", [128, D_A], F32)
            for j in range(2):
                self.V(lambda: nc.vector.reciprocal(out=rn[:, j * 512:(j + 1) * 512], in_=ps[4 + j][:, :]),
                       [psb[4 + j]], [rnb])
            ny, nyb = self.sb(es, "ny", [1, D_A], F32)
            for j in range(2):
                self.mm_group(ps[6][0:1, :], psb[6],
                              [(alt[:, 0:1], hsT[:, i, j * 512:(j + 1) * 512]) for i in range(LC)], [altb, hsTb])
                self.tt(ny[:, j * 512:(j + 1) * 512], ps[6][0:1, :], rn[0:1, j * 512:(j + 1) * 512], ALU.mult,
                        [psb[6], rnb], [nyb])
            self.tt(ny[:], ny[:], hb[0:1, :], ALU.add, [nyb, hbb], [nyb])
            self.barrier()
            tmp, tmpb = self.sb(es, "ktmp", [128, 512], F32)
            halves = [(0, 512), (512, 512)]
            chunks = []
            for (tab, src, srcb, dst, dstb, addb) in (("Fc", hsT, hsTb, Kre, Kreb, True),
                                                      ("Fs", hdT, hdTb, Kim, Kimb, False)):
                chunks += [dict(act=(src, srcb), kc=LC, M=128, tag=(f, dst, dstb, addb),
                                pieces=[(C[f"{tab}{L}"][f], 128)]) for f in range(LC)]

            def on_tile(ci, ch, ti, t0, n, ps_ap, psb_):
                f, dst, dstb, addb = ch["tag"]
                if addb:
                    self.tt(tmp[:, 0:n], ps_ap, rn[:, t0:t0 + n], ALU.mult, [psb_, rnb], [tmpb])
                    self.tt(dst[:, f, t0:t0 + n], tmp[:, 0:n], hb[:, t0:t0 + n], ALU.add, [tmpb, hbb], [dstb])
                else:
                    self.tt(dst[:, f, t0:t0 + n], ps_ap, rn[:, t0:t0 + n], ALU.mult, [psb_, rnb], [dstb])

            self.gemm(chunks, halves, on_tile, kcmax=LC, cast=False)
            self.V(lambda: nc.vector.tensor_copy(out=Kim[0:1, 0, :], in_=ny[:]), [nyb], [Kimb])
            self.barrier()

    def hy_conv(self, L, LC, C, blocks, Kre, Kreb, Kim, Kimb, X0, ZZ, YA):
        nc = self.nc
        sp = self.sp
        ps, psb = self.ps, self.psb
        NB = len(blocks)
        with ExitStack() as es:
            Yre, Yreb = self.sb(es, "Yre", [128, LC, NB * 512], BF16)
            Yim, Yimb = self.sb(es, "Yim", [128, LC, NB * 512], BF16)
            with ExitStack() as es2:
                zzT, zzTb = self.sb(es2, "zzT", [128, LC, NB * 512], BF16)
                zzr = self.sbring(es2, "zz", [128, 4, L], BF16, 2)
                for bi, (s0, c0) in enumerate(blocks):
                    zz, zzb = zzr.next()
                    sp.dma(zz[:], ZZ[0][c0:c0 + 512, s0:s0 + L].rearrange("(c p) t -> p c t", p=128), reads=[ZZ[1]],
                           writes=[zzb])
                    for i in range(LC):
                        self.pe.acquire([zzb, self.ident_b], [self.pstb])
                        ins = None
                        for c in range(4):
                            ins = nc.tensor.transpose(self.pst[:, c * 128:(c + 1) * 128],
                                                      zz[:, c, i * 128:(i + 1) * 128], self.ident[:])
                        tok = self.pe.stamp(ins)
                        self.pe.release(tok, [zzb, self.ident_b], [self.pstb])
                        self.act_fn(zzT[:, i, bi * 512:(bi + 1) * 512], self.pst[:, 0:512], AF.Copy, [self.pstb],
                                    [zzTb])
                zre, zreb = self.sb(es2, "zre", [128, NB * 512], F32)
                zimr = self.sbring(es2, "zim", [128, 512], F32, 2)
                ta, tab_ = self.sb(es2, "ta", [128, 512], F32)
                tb, tbb = self.sb(es2, "tb", [128, 512], F32)
                chunks = []
                for f in range(LC):
                    for nm in ("Fc", "Fs"):
                        chunks.append(dict(act=(zzT, zzTb), kc=LC, M=128, tag=(nm, f),
                                           pieces=[(C[f"{nm}{L}"][f], 128)]))

                def on_tile(ci, ch, ti, t0, n, ps_ap, psb_):
                    nm, f = ch["tag"]
                    c0 = blocks[ti][1]
                    zr = zre[:, t0:t0 + 512]
                    if nm == "Fc":
                        self.act_fn(zr, ps_ap, AF.Copy, [psb_], [zreb])
                    else:
                        zim, zimb = zimr.next()
                        self.act_fn(zim[:], ps_ap, AF.Copy, [psb_], [zimb])
                        kr = Kre[:, f, c0:c0 + 512]
                        ki = Kim[:, f, c0:c0 + 512]
                        yre = Yre[:, f, t0:t0 + 512]
                        yim = Yim[:, f, t0:t0 + 512]
                        self.tt(ta[:], zr, kr, ALU.mult, [zreb, Kreb], [tab_])
                        self.tt(tb[:], zim[:], ki, ALU.mult, [zimb, Kimb], [tbb])
                        self.tt(yre, ta[:], tb[:], ALU.subtract, [tab_, tbb], [Yreb])
                        if f == 0:
                            self.V(lambda: nc.vector.tensor_copy(out=yre[0:1, :], in_=ta[0:1, :]), [tab_], [Yreb])
                        self.tt(ta[:], zr, ki, ALU.mult, [zreb, Kimb], [tab_])
                        if f == 0:
                            self.V(lambda: nc.vector.tensor_copy(out=zr[0:1, :], in_=tb[0:1, :]), [tbb], [zreb])
                        self.tt(tb[:], zim[:], kr, ALU.mult, [zimb, Kreb], [tbb])
                        self.tt(yim, ta[:], tb[:], ALU.add, [tab_, tbb], [Yimb])
                        if f == 0:
                            self.V(lambda: nc.vector.tensor_copy(out=yim[0:1, :], in_=zr[0:1, :]), [zreb], [Yimb])

                self.gemm(chunks, [(bi * 512, 512) for bi in range(NB)], on_tile, kcmax=LC, cast=False)
            with ExitStack() as es3:
                gcr = self.sbring(es3, "gc", [128, LC, 512], BF16, 2)
                gsr = self.sbring(es3, "gs", [128, LC, 512], BF16, 2)
                x0r = self.sbring(es3, "x0", [128, 4, 512], BF16, 3)
                yor = self.sbring(es3, "yo", [128, 512], BF16, 3)
                pr = Ring([0, 1, 2, 3])
                ttiles = [(t0, min(512, L - t0)) for t0 in range(0, L, 512)]
                gl = {}
                xl = {}
                work = [(ti, bi) for ti in range(len(ttiles)) for bi in range(NB)]

                def load_g(ti):
                    if ti < len(ttiles) and ti not in gl:
                        t0, n = ttiles[ti]
                        gct, gcb = gcr.next()
                        gst, gsb = gsr.next()
                        sp.dma(gct[:, :, 0:n], C[f"Gc{L}"][t0 // 512], writes=[gcb])
                        sp.dma(gst[:, :, 0:n], C[f"Gs{L}"][t0 // 512], writes=[gsb])
                        gl[ti] = (gct, gcb, gst, gsb)

                def load_x0(wi):
                    if wi < len(work) and wi not in xl:
                        ti, bi = work[wi]
                        t0, n = ttiles[ti]
                        s0, c0 = blocks[bi]
                        x0t, x0b = x0r.next()
                        sp.dma(x0t[:, :, 0:n],
                               X0[0][c0:c0 + 512, s0 + t0:s0 + t0 + n].rearrange("(c p) t -> p c t", p=128),
                               reads=[X0[1]], writes=[x0b])
                        xl[wi] = (x0t, x0b)

                load_g(0)
                load_x0(0)
                for wi, (ti, bi) in enumerate(work):
                    t0, n = ttiles[ti]
                    s0, c0 = blocks[bi]
                    if bi == 0:
                        load_g(ti + 1)
                    load_x0(wi + 1)
                    gct, gcb, gst, gsb = gl[ti]
                    x0t, x0b = xl.pop(wi)
                    for c in range(4):
                        p = pr.next()
                        cs = bi * 512 + c * 128
                        pairs = [(Yre[:, f, cs:cs + 128], gct[:, f, 0:n]) for f in range(LC)]
                        pairs += [(Yim[:, f, cs:cs + 128], gst[:, f, 0:n]) for f in range(LC)]
                        self.mm_group(ps[p][:, 0:n], psb[p], pairs, [Yreb, Yimb, gcb, gsb])
                        yt, ytb = yor.next()
                        self.tt(yt[:, 0:n], ps[p][:, 0:n], x0t[:, c, 0:n], ALU.mult, [psb[p], x0b], [ytb])
                        sp.dma(YA[0][c0 + c * 128:c0 + (c + 1) * 128, s0 + t0:s0 + t0 + n], yt[:, 0:n], reads=[ytb],
                               writes=[YA[1]])
                self.barrier()

    def merge(self, l, W, SIG, YA, YB, YC, MM):
        nc = self.nc
        sp = self.sp
        with ExitStack() as es:
            ys = []
            ytl = []
            for (nm, scr) in (("a", YA), ("b", YB), ("c", YC)):
                yt, _ = self.sb(es, f"y{nm}", [128, 8, T], BF16)
                ytl.append((yt, scr, [Buf(f"y{nm}{i}") for i in range(len(TILES))]))
            for i, (t0, n) in enumerate(TILES):
                for (yt, scr, bufs) in ytl:
                    sp.dma(yt[:, :, t0:t0 + n], scr[0][:, t0:t0 + n].rearrange("(c p) t -> p c t", p=128),
                           reads=[scr[1]], writes=[bufs[i]])
            for (yt, scr, bufs) in ytl:
                ys.append((yt, bufs))
            sgr = self.sbring(es, "sg", [128, T], BF16, 4)
            accr = self.sbring(es, "acc", [128, T], F32, 2)
            tmp, tmpb = self.sb(es, "mtmp", [128, 512], F32)
            mor = self.sbring(es, "mo", [128, T], BF16, 2)
            chunks = []
            for j in range(DC):
                for bi, nm in enumerate(("w_br_a", "w_br_b", "w_br_c")):
                    chunks.append(dict(act=ys[bi], kc=8, M=128, tag=(j, bi),
                                       pieces=[(self.wview(W[nm][l], 8, j * 128, 128), 128)]))
            st = {}

            sgl = {}

            def load_sig(ci_):
                if ci_ < len(chunks) and ci_ not in sgl:
                    j_, bi_ = chunks[ci_]["tag"]
                    sgt_, sgb_ = sgr.next()
                    sp.dma(sgt_[:], SIG[0][(bi_ * DC + j_) * 128:(bi_ * DC + j_ + 1) * 128, :], reads=[SIG[1]],
                           writes=[sgb_])
                    sgl[ci_] = (sgt_, sgb_)

            def on_tile(ci, ch, ti, t0, n, ps_ap, psb_):
                j, bi = ch["tag"]
                if ti == 0:
                    load_sig(ci)
                    load_sig(ci + 1)
                    load_sig(ci + 2)
                    st["sg"] = sgl.pop(ci)
                    if bi == 0:
                        st["acc"] = accr.next()
                sgt, sgb = st["sg"]
                at, ab = st["acc"]
                if bi == 0:
                    self.tt(at[:, t0:t0 + n], ps_ap, sgt[:, t0:t0 + n], ALU.mult, [psb_, sgb], [ab])
                else:
                    self.tt(tmp[:, 0:n], ps_ap, sgt[:, t0:t0 + n], ALU.mult, [psb_, sgb], [tmpb])
                    self.tt(at[:, t0:t0 + n], at[:, t0:t0 + n], tmp[:, 0:n], ALU.add, [ab, tmpb], [ab])

            def on_chunk(ci, ch):
                j, bi = ch["tag"]
                if bi == 2:
                    at, ab = st["acc"]
                    mt, mb = mor.next()
                    self.act_fn(mt[:], at[:], AF.Copy, [ab], [mb])
                    sp.dma(MM[0][j * 128:(j + 1) * 128, :], mt[:], reads=[mb], writes=[MM[1]])

            self.gemm(chunks, TILES, on_tile, on_chunk, kcmax=8)

    def outproj(self, w2d, KC, SRC, OO, tile_groups):
        nc = self.nc
        sp = self.sp
        for tg in tile_groups:
            a0 = tg[0][0]
            a1 = tg[-1][0] + tg[-1][1]
            with ExitStack() as es:
                at, _ = self.sb(es, "opa", [128, KC, a1 - a0], BF16)
                ab = [Buf(f"opa{i}") for i in range(len(tg))]
                for i, (t0_, n_) in enumerate(tg):
                    sp.dma(at[:, :, t0_ - a0:t0_ - a0 + n_],
                           SRC[0][:, t0_:t0_ + n_].rearrange("(c p) t -> p c t", p=128), reads=[SRC[1]],
                           writes=[ab[i]])
                otr = self.sbring(es, "opo", [128, 512], BF16, 4)
                sqr = self.sbring(es, "opsq", [128, 512], BF16, 4)
                chunks = [dict(act=(at, ab), kc=KC, M=128, tag=j,
                               pieces=[(self.wview(w2d, KC, j * 128, 128), 128)]) for j in range(DC)]
                tiles = [(t0 - a0, n) for (t0, n) in tg]
                pending = []

                def emit_acc():
                    j, gti, sqt, sqb, n = pending.pop(0)
                    pe = self.pe
                    bank = 2 + gti
                    pe.acquire([sqb, self.ones_b], [self.psb[bank]])
                    ins = nc.tensor.matmul(self.ps[bank][:, 0:n], self.ones[:], sqt[:, 0:n], start=(j == 0),
                                           stop=(j == DC - 1))
                    tok = pe.stamp(ins)
                    pe.release(tok, [sqb, self.ones_b], [self.psb[bank]])

                def on_tile(ci, ch, ti, t0, n, ps_ap, psb_, a0=a0):
                    j = ch["tag"]
                    while len(pending) >= 2:
                        emit_acc()
                    ot, ob = otr.next()
                    self.act_fn(ot[:, 0:n], ps_ap, AF.Copy, [psb_], [ob])
                    sp.dma(OO[0][j * 128:(j + 1) * 128, a0 + t0:a0 + t0 + n], ot[:, 0:n], reads=[ob],
                           writes=[OO[1]])
                    sqt, sqb = sqr.next()
                    self.act_fn(sqt[:, 0:n], ps_ap, AF.Square, [psb_], [sqb])
                    pending.append((j, (a0 + t0) // 512, sqt, sqb, n))

                def on_end():
                    while pending:
                        emit_acc()

                self.gemm(chunks, tiles, on_tile, kcmax=KC, banks=[0, 1], on_end=on_end)

    def ffn_up(self, es, l, hT, hTb, w_up, FF, bg=None):
        nc = self.nc
        sp = self.sp
        NCH = D_FF // 128
        chunks = []
        for c in range(NCH):
            chunks.append(dict(act=(hT, hTb), kc=DC, M=128, tag=("g", c),
                               pieces=[(self.wview(w_up, DC, c * 128, 128), 128)]))
            chunks.append(dict(act=(hT, hTb), kc=DC, M=128, tag=("v", c),
                               pieces=[(self.wview(w_up, DC, D_FF + c * 128, 128), 128)]))
        stg = self.sbring(es, "fstg", [128, T], F32, 3)
        cvr = self.sbring(es, "fcv", [128, T], F32, 3)
        obr = self.sbring(es, "fob", [128, T], BF16, 2)
        st = {}

        def on_tile(ci, ch, ti, t0, n, ps_ap, psb_):
            if ti == 0:
                st["stg"] = stg.next()
            sg, sgb = st["stg"]
            self.act_fn(sg[:, t0:t0 + n], ps_ap, AF.Copy, [psb_], [sgb])

        def on_chunk(ci, ch):
            kind, c = ch["tag"]
            col = c if kind == "g" else NCH + c
            sg, sgb = st["stg"]
            cvt, cvb = cvr.next()
            self.conv3(cvt[:], cvb, sg[:], sgb, self.vc(l, "ffn_w0", col), self.vc(l, "ffn_w1", col),
                       self.vc(l, "ffn_w2", col), self.vc(l, "ffn_b", col))
            if kind == "g":
                self.act_fn(cvt[:], cvt[:], AF.Silu, [cvb], [cvb])
                st["g"] = (cvt, cvb)
            else:
                gt, gb = st["g"]
                ot, ob = obr.next()
                self.tt(ot[:], gt[:], cvt[:], ALU.mult, [gb, cvb], [ob])
                sp.dma(FF[0][c * 128:(c + 1) * 128, :], ot[:], reads=[ob], writes=[FF[1]])

        self.gemm(chunks, TILES, on_tile, on_chunk, bg=bg)


_CACHE = {}


def _consts():
    if "c" in _CACHE:
        return _CACHE["c"]
    bf = ml_dtypes.bfloat16
    c = {}
    for L in (TS, LP):
        idx = np.arange(L, dtype=np.float64)
        ang = np.pi * np.outer(idx, idx) / L
        Fc = np.cos(ang)
        Fs = -np.sin(ang)
        Fs[:, 0] = (-1.0) ** idx
        Gc = np.cos(ang) / L
        Gc[0, :] = 1.0 / (2 * L)
        Gs = -np.sin(ang) / L
        Gs[0, :] = ((-1.0) ** idx) / (2 * L)
        LCh = L // 128

        def tile_f(M):
            return np.ascontiguousarray(M.astype(np.float32).reshape(LCh, 128, LCh, 128).transpose(2, 1, 0, 3)).astype(bf)

        def tile_g(M):
            w_ = min(512, L)
            return np.ascontiguousarray(M.astype(np.float32).reshape(LCh, 128, L // w_, w_).transpose(2, 1, 0, 3)).astype(bf)

        c[f"Fc{L}"] = tile_f(Fc)
        c[f"Fs{L}"] = tile_f(Fs)
        c[f"Gc{L}"] = tile_g(Gc)
        c[f"Gs{L}"] = tile_g(Gs)
        t_idx = np.arange(L, dtype=np.float32)
        t_norm = t_idx / np.float32(max(L - 1, 1))
        w = (np.float32(2.0 * math.pi) * t_idx / np.float32(L)).astype(np.float32)
        bands = np.linspace(1e-4, 15, 16, dtype=np.float32)
        angz = (w[:, None] * bands[None, :]).astype(np.float32)
        z = np.concatenate([t_norm[:, None], np.cos(angz), -np.sin(angz)], axis=-1).astype(np.float32)
        c[f"zT{L}"] = np.ascontiguousarray(z.T)
        c[f"negtn{L}"] = np.ascontiguousarray((-t_norm).reshape(L // 128, 128).T).astype(np.float32)
    max_decay = math.log(1e-2) / 0.3
    min_decay = math.log(1e-2) / 1.5
    deltas = np.abs(np.linspace(min_decay, max_decay, D_A, dtype=np.float32))
    c["delta_b"] = np.ascontiguousarray(np.broadcast_to(deltas[None, :], (128, D_A))).astype(np.float32)
    c["alt"] = (((-1.0) ** np.arange(128)).reshape(128, 1)).astype(np.float32).astype(bf)
    rows = TS // 64
    row = np.repeat(np.arange(rows, dtype=np.float32), 64)
    col = np.tile(np.arange(64, dtype=np.float32), rows)
    inv = (np.float32(10000.0) ** (-np.arange(16, dtype=np.float32) / np.float32(16))).astype(np.float32)
    ang = np.concatenate([row[:, None] * inv, col[:, None] * inv], axis=-1).astype(np.float32)
    cos, sin = np.cos(ang).T, np.sin(ang).T
    CC = np.ones((ROPE, T), np.float32)
    SS = np.zeros((ROPE, T), np.float32)
    CC[0:32, 0:TS] = cos
    CC[32:64, 0:TS] = cos
    SS[0:32, 0:TS] = -sin
    SS[32:64, 0:TS] = sin
    c["CC"], c["SS"] = CC, SS
    c["ident"] = np.eye(128, dtype=np.float32).astype(bf)
    _CACHE["c"] = c
    return c


def _vec_table(inp):
    tab = np.zeros((128, DEPTH, NVEC), np.float32)
    for l in range(DEPTH):
        src = {"norm_mix_pre": inp["norm_mix_pre"][l], "norm_mix_post": inp["norm_mix_post"][l],
               "norm_ffn_pre": inp["norm_ffn_pre"][l], "norm_ffn_post": inp["norm_ffn_post"][l],
               "ada_b": inp["ada_b"][l], "hy_conv_w0": inp["hy_conv_w"][l, 0], "hy_conv_w1": inp["hy_conv_w"][l, 1],
               "hy_conv_w2": inp["hy_conv_w"][l, 2], "hy_conv_b": inp["hy_conv_b"][l], "q_norm": inp["q_norm"][l],
               "kv_norm": inp["kv_norm"][l], "sc_w0": inp["sc_conv_w"][l, 0], "sc_w1": inp["sc_conv_w"][l, 1],
               "sc_w2": inp["sc_conv_w"][l, 2], "ffn_w0": inp["ffn_conv_w"][l, 0], "ffn_w1": inp["ffn_conv_w"][l, 1],
               "ffn_w2": inp["ffn_conv_w"][l, 2], "ffn_b": inp["ffn_conv_b"][l]}
        for name, k in VEC_SPECS:
            v = np.asarray(src[name], np.float32).reshape(k, 128)
            tab[:, l, VCOL[name]:VCOL[name] + k] = v.T
    return tab


def host_inputs(inp):
    inp = {k: np.asarray(v) for k, v in inp.items()}
    cst = _consts()
    shared = dict(cst)
    for nm in ("ada_w", "w_in", "hy_f_w1", "hy_f_w2", "hy_f_w3", "hy_f_b3", "w_uq", "w_ukv", "w_br_a", "w_br_b",
               "w_br_c", "w_o", "ffn_up", "ffn_down"):
        shared[nm] = np.ascontiguousarray(inp[nm], dtype=np.float32)
    shared["vecT"] = _vec_table(inp)
    hyvec = np.stack([inp["hy_f_b1"], inp["hy_f_b2"], inp["hy_f_freq"][:, 0], inp["hy_f_freq"][:, 1]], axis=-1)
    shared["hyvec"] = np.ascontiguousarray(hyvec, dtype=np.float32)
    shared["hy_bias"] = np.ascontiguousarray(
        np.broadcast_to(inp["hy_bias"][:, None, :], (DEPTH, 128, D_A)), dtype=np.float32)
    maps = []
    for c in range(8):
        m = dict(shared)
        xs = inp["x_sample"][c].T
        xp0 = inp["x_prompt"][2 * c].T
        xp1 = inp["x_prompt"][2 * c + 1].T
        m["xT_in"] = np.ascontiguousarray(np.concatenate([xs, xp0, xp1], axis=1), dtype=np.float32)
        cc = np.stack([inp["c"][c], inp["c_ctx"]], axis=-1)
        m["cT"] = np.ascontiguousarray(cc.reshape(DC, 128, 2).transpose(1, 0, 2), dtype=np.float32)
        ck = inp["cache_ckv"][c]
        m["cckvT"] = np.ascontiguousarray(ck.transpose(0, 2, 1).reshape(DEPTH, 2, 128, PAST).transpose(0, 2, 1, 3),
                                          dtype=np.float32)
        kp = inp["cache_kpe"][c]
        m["ckpeT"] = np.ascontiguousarray(kp.transpose(0, 2, 1), dtype=np.float32)
        maps.append(m)
    return maps


def assemble(results):
    y_prompt = np.zeros((16, LP, D), np.float32)
    y_sample = np.zeros((8, TS, D), np.float32)
    new_ckv = np.zeros((16, DEPTH, LP, KVL), np.float32)
    new_kpe = np.zeros((16, DEPTH, LP, ROPE), np.float32)
    for c in range(8):
        r = results[c]
        yT = np.asarray(r["yT"])
        y_sample[c] = yT[:, 0:TS].T
        y_prompt[2 * c] = yT[:, TS:TS + LP].T
        y_prompt[2 * c + 1] = yT[:, TS + LP:T].T
        ck = np.asarray(r["ckv_out"])
        kp = np.asarray(r["kpe_out"])
        for j in range(2):
            new_ckv[2 * c + j] = ck[:, :, j * LP:(j + 1) * LP].transpose(0, 2, 1)
            new_kpe[2 * c + j] = kp[:, :, j * LP:(j + 1) * LP].transpose(0, 2, 1)
    return (y_prompt, y_sample, new_ckv, new_kpe)


def kernel(**inputs):
    if "nc" not in _CACHE:
        _CACHE["nc"] = MK().build()
    nc = _CACHE["nc"]
    maps = host_inputs(inputs)
    res = run_bass_kernel_spmd(nc, maps, core_ids=list(range(8)))
    return assemble(res.results)
```
